# Optimizing a Trainium2 kernel written in Bass

```python
import jax, jax.numpy as jnp
from jax import lax
import numpy as np

D_MODEL = 2048
BATCH = 4
SEQ = 4096
DEPTH = 1

GRID_W = 64
N_MEM = 256
HEAD_DIM = 128
NA_HEADS = 8
NA_WIN_ROWS = 8
NA_WIN_COLS = 16
SG_GROUPS = 4
SG_CHUNK = 128
MEM_HEADS = 4
D_NA = NA_HEADS * HEAD_DIM
D_SG = SG_GROUPS * HEAD_DIM
D_MEM = MEM_HEADS * HEAD_DIM
D_MIX = D_NA + D_SG + D_MEM
D_IN = 3 * D_NA + 2 * D_SG + D_MEM
D_FF = 5632
EPS = 1e-6
NEG_INF = -1e30

kernel_name = "hybrid_natten_gmlp_memxattn_macaron"


def _rmsnorm(x, g):
    xf = x.astype(jnp.float32)
    y = xf * lax.rsqrt(jnp.mean(xf * xf, axis=-1, keepdims=True) + EPS)
    return (y * g.astype(jnp.float32)).astype(x.dtype)


def _layernorm(x, g, b):
    xf = x.astype(jnp.float32)
    mu = jnp.mean(xf, axis=-1, keepdims=True)
    var = jnp.mean(jnp.square(xf - mu), axis=-1, keepdims=True)
    y = (xf - mu) * lax.rsqrt(var + EPS)
    return (y * g.astype(jnp.float32) + b.astype(jnp.float32)).astype(x.dtype)


def _swiglu(x, w_gate_up, w_down):
    gu = x @ w_gate_up
    g, u = jnp.split(gu, 2, axis=-1)
    return (jax.nn.silu(g) * u) @ w_down


def _neighbourhood_attention(q, k, v, rpb):
    B, T, H, Dh = q.shape
    rows = T // GRID_W
    kh = min(NA_WIN_ROWS, rows)
    qg = q.reshape(B, rows, GRID_W, H, Dh)
    kg = k.reshape(B, rows, GRID_W, H, Dh)
    vg = v.reshape(B, rows, GRID_W, H, Dh)
    r = jnp.arange(rows)
    row_start = jnp.clip(r - kh // 2, 0, rows - kh)
    key_rows = row_start[:, None] + jnp.arange(kh)[None, :]
    k_blk = kg[:, key_rows]
    v_blk = vg[:, key_rows]
    c = jnp.arange(GRID_W)
    col_start = jnp.clip(c - NA_WIN_COLS // 2, 0, GRID_W - NA_WIN_COLS)
    col_in = (c[None, :] >= col_start[:, None]) & (c[None, :] < col_start[:, None] + NA_WIN_COLS)
    dr = key_rows - r[:, None]
    dc = jnp.clip(c[None, :] - c[:, None], -(NA_WIN_COLS - 1), NA_WIN_COLS - 1)
    bias = rpb[:, dr[:, None, :, None] + (NA_WIN_ROWS - 1),
               dc[None, :, None, :] + (NA_WIN_COLS - 1)]
    scale = Dh ** -0.5
    s = jnp.einsum('brqhd,brikhd->bhrqik', qg, k_blk).astype(jnp.float32) * scale
    s = s + bias[None].astype(jnp.float32)
    s = jnp.where(col_in[None, None, None, :, None, :], s, NEG_INF)
    p = jax.nn.softmax(s.reshape(B, H, rows, GRID_W, kh * GRID_W), axis=-1)
    p = p.reshape(B, H, rows, GRID_W, kh, GRID_W).astype(v.dtype)
    o = jnp.einsum('bhrqik,brikhd->brqhd', p, v_blk)
    return o.reshape(B, T, H * Dh)


def _spatial_gating(z, ln_g, ln_b, w_s, b_s):
    B, T, _ = z.shape
    n = T // SG_CHUNK
    u, vv = jnp.split(z, 2, axis=-1)
    vv = vv.reshape(B, n, SG_CHUNK, SG_GROUPS, HEAD_DIM)
    vv = _layernorm(vv, ln_g, ln_b)
    mixed = jnp.einsum('gpq,bnqgc->bnpgc', w_s, vv) + b_s.T[None, None, :, :, None]
    return u * mixed.reshape(B, T, D_SG)


def _memory_attention(q, mem_n, w_mem_kv):
    B, M, _ = mem_n.shape
    kv = mem_n @ w_mem_kv
    km, vm = jnp.split(kv, 2, axis=-1)
    km = km.reshape(B, M, MEM_HEADS, HEAD_DIM)
    vm = vm.reshape(B, M, MEM_HEADS, HEAD_DIM)
    s = jnp.einsum('bthd,bmhd->bhtm', q, km).astype(jnp.float32) * (HEAD_DIM ** -0.5)
    p = jax.nn.softmax(s, axis=-1).astype(vm.dtype)
    o = jnp.einsum('bhtm,bmhd->bthd', p, vm)
    return o.reshape(q.shape[0], q.shape[1], D_MEM)


def _mixer(xn, mem_n, w_in, w_mem_kv, na_rpb, sg_ln_gain, sg_ln_bias, sg_w_spatial,
           sg_b_spatial, out_norm_na, out_norm_sg, out_norm_mem, w_out):
    B, T, _ = xn.shape
    proj = xn @ w_in
    q_na, k_na, v_na, z_sg, q_mem = jnp.split(
        proj, [D_NA, 2 * D_NA, 3 * D_NA, 3 * D_NA + 2 * D_SG], axis=-1)
    shp = (B, T, NA_HEADS, HEAD_DIM)
    y_na = _neighbourhood_attention(q_na.reshape(shp), k_na.reshape(shp), v_na.reshape(shp), na_rpb)
    y_sg = _spatial_gating(jax.nn.gelu(z_sg), sg_ln_gain, sg_ln_bias, sg_w_spatial, sg_b_spatial)
    y_mem = _memory_attention(q_mem.reshape(B, T, MEM_HEADS, HEAD_DIM), mem_n, w_mem_kv)
    y = jnp.concatenate([_rmsnorm(y_na, out_norm_na),
                         _rmsnorm(y_sg, out_norm_sg),
                         _rmsnorm(y_mem, out_norm_mem)], axis=-1)
    return y @ w_out


def setup_inputs(seed: int = 0) -> dict:
    key = jax.random.key(seed)
    ks = jax.random.split(key, 32)
    L = DEPTH

    def nrm(k, shape, scale):
        return jax.random.normal(k, shape, jnp.float32) * scale

    def gain(k, shape):
        return 1.0 + 0.05 * jax.random.normal(k, shape, jnp.float32)

    return {
        "x": nrm(ks[0], (BATCH, SEQ, D_MODEL), 1.0),
        "mem": nrm(ks[1], (BATCH, N_MEM, D_MODEL), 1.0),
        "ffn1_norm_pre": gain(ks[2], (L, D_MODEL)),
        "ffn1_w_gate_up": nrm(ks[3], (L, D_MODEL, 2 * D_FF), D_MODEL ** -0.5),
        "ffn1_w_down": nrm(ks[4], (L, D_FF, D_MODEL), D_FF ** -0.5),
        "ffn1_norm_post": gain(ks[5], (L, D_MODEL)),
        "mix_norm_pre": gain(ks[6], (L, D_MODEL)),
        "mem_norm": gain(ks[7], (L, D_MODEL)),
        "w_in": nrm(ks[8], (L, D_MODEL, D_IN), D_MODEL ** -0.5),
        "w_mem_kv": nrm(ks[9], (L, D_MODEL, 2 * D_MEM), D_MODEL ** -0.5),
        "na_rpb": nrm(ks[10], (L, NA_HEADS, 2 * NA_WIN_ROWS - 1, 2 * NA_WIN_COLS - 1), 0.1),
        "sg_ln_gain": gain(ks[11], (L, SG_GROUPS, HEAD_DIM)),
        "sg_ln_bias": nrm(ks[12], (L, SG_GROUPS, HEAD_DIM), 0.02),
        "sg_w_spatial": nrm(ks[13], (L, SG_GROUPS, SG_CHUNK, SG_CHUNK), SG_CHUNK ** -0.5),
        "sg_b_spatial": nrm(ks[14], (L, SG_GROUPS, SG_CHUNK), 0.02),
        "out_norm_na": gain(ks[15], (L, D_NA)),
        "out_norm_sg": gain(ks[16], (L, D_SG)),
        "out_norm_mem": gain(ks[17], (L, D_MEM)),
        "w_out": nrm(ks[18], (L, D_MIX, D_MODEL), D_MIX ** -0.5),
        "mix_norm_post": gain(ks[19], (L, D_MODEL)),
        "ffn2_norm_pre": gain(ks[20], (L, D_MODEL)),
        "ffn2_w_gate_up": nrm(ks[21], (L, D_MODEL, 2 * D_FF), D_MODEL ** -0.5),
        "ffn2_w_down": nrm(ks[22], (L, D_FF, D_MODEL), D_FF ** -0.5),
        "ffn2_norm_post": gain(ks[23], (L, D_MODEL)),
        "final_norm": gain(ks[24], (L, D_MODEL)),
    }


def reference(x, mem, ffn1_norm_pre, ffn1_w_gate_up, ffn1_w_down, ffn1_norm_post,
              mix_norm_pre, mem_norm, w_in, w_mem_kv, na_rpb, sg_ln_gain, sg_ln_bias,
              sg_w_spatial, sg_b_spatial, out_norm_na, out_norm_sg, out_norm_mem, w_out,
              mix_norm_post, ffn2_norm_pre, ffn2_w_gate_up, ffn2_w_down, ffn2_norm_post,
              final_norm):
    h = x
    for l in range(DEPTH):
        f = _swiglu(_rmsnorm(h, ffn1_norm_pre[l]), ffn1_w_gate_up[l], ffn1_w_down[l])
        h = h + 0.5 * _rmsnorm(f, ffn1_norm_post[l])
        xn = _rmsnorm(h, mix_norm_pre[l])
        mem_n = _rmsnorm(mem, mem_norm[l])
        m = _mixer(xn, mem_n, w_in[l], w_mem_kv[l], na_rpb[l], sg_ln_gain[l], sg_ln_bias[l],
                   sg_w_spatial[l], sg_b_spatial[l], out_norm_na[l], out_norm_sg[l],
                   out_norm_mem[l], w_out[l])
        h = h + _rmsnorm(m, mix_norm_post[l])
        f = _swiglu(_rmsnorm(h, ffn2_norm_pre[l]), ffn2_w_gate_up[l], ffn2_w_down[l])
        h = h + 0.5 * _rmsnorm(f, ffn2_norm_post[l])
        h = _rmsnorm(h, final_norm[l])
    return h
```

```python
import contextlib
import numpy as np
import concourse.bass as bass
import concourse.mybir as mybir
from concourse.bass_utils import run_bass_kernel_spmd

F32 = mybir.dt.float32
BF16 = mybir.dt.bfloat16
AF = mybir.ActivationFunctionType
ALU = mybir.AluOpType

D = 2048
DFF = 5632
NCH = DFF // 128
NK = D // 128
NT = 2304
NOWN = 2048
NMEM = 256
EPS = 1e-6
NEG = -30000.0
G_F1PRE, G_F1POST, G_MIXPRE, G_MEM, G_OUTN, G_MIXPOST, G_F2PRE, G_F2POST, G_FINAL = range(9)


class Op:
    __slots__ = ("idx", "eng", "fn", "deps", "dkey", "dval", "sig", "sigval", "bar")


class Prog:
    ENGS = ("pe", "act", "dve", "pool", "sp")

    def __init__(self, same_engine_sync=True):
        self.ops = []
        self.last_w = {}
        self.readers = {}
        self.dma_count = {}
        self.same_engine_sync = same_engine_sync

    def add(self, eng, fn, reads=(), writes=(), dkey=None):
        op = Op()
        op.idx = len(self.ops)
        op.eng = eng
        op.fn = fn
        op.dkey = dkey
        op.bar = False
        op.sig = False
        op.sigval = 0
        deps = set()
        for r in reads:
            w = self.last_w.get(r)
            if w is not None:
                deps.add(w)
        for r in writes:
            w = self.last_w.get(r)
            if w is not None:
                deps.add(w)
            rd = self.readers.get(r)
            if rd:
                deps.update(rd.values())
        key = eng if dkey is None else ("dma", op.idx)
        for r in reads:
            self.readers.setdefault(r, {})[key] = op.idx
        for r in writes:
            self.last_w[r] = op.idx
            self.readers[r] = {}
        deps.discard(op.idx)
        op.deps = deps
        if dkey is not None:
            c = self.dma_count.get(dkey, 0) + 1
            self.dma_count[dkey] = c
            op.dval = 16 * c
        else:
            op.dval = 0
        self.ops.append(op)
        return op

    def barrier(self):
        op = Op()
        op.idx = len(self.ops)
        op.eng = None
        op.fn = None
        op.dkey = None
        op.bar = True
        op.deps = set()
        op.sig = False
        op.sigval = 0
        op.dval = 0
        self.ops.append(op)
        self.last_w = {}
        self.readers = {}

    def emit(self, nc):
        ops = self.ops
        last_on_eng = {}
        bar_snap = {}
        dma_state = {}
        for op in ops:
            if op.bar:
                for e, o in last_on_eng.items():
                    o.sig = True
                bar_snap[op.idx] = (dict(last_on_eng), dict(dma_state))
                continue
            for d in op.deps:
                dep = ops[d]
                if dep.dkey is not None:
                    continue
                if dep.eng == op.eng and (op.eng == "pe" or not self.same_engine_sync):
                    continue
                dep.sig = True
            if op.dkey is not None:
                dma_state[op.dkey] = op.dval
            else:
                last_on_eng[op.eng] = op
        cnt = {e: 0 for e in self.ENGS}
        for op in ops:
            if op.bar or op.dkey is not None:
                continue
            if op.sig:
                cnt[op.eng] += 1
                op.sigval = cnt[op.eng]
        per_eng = {e: [] for e in self.ENGS}
        for op in ops:
            if op.bar:
                for e in self.ENGS:
                    per_eng[e].append(op)
            else:
                per_eng[op.eng].append(op)
        dkeys = sorted(self.dma_count.keys(), key=str)
        with contextlib.ExitStack() as st:
            esem = {e: st.enter_context(nc.semaphore("se_" + e)) for e in self.ENGS}
            dsem = {k: st.enter_context(nc.semaphore("sd_%d" % i)) for i, k in enumerate(dkeys)}
            block = st.enter_context(nc.Block())

            def run_engine(ename, e):
                waited = {}

                def wait(sem, val, key):
                    if val <= 0 or waited.get(key, 0) >= val:
                        return
                    waited[key] = val
                    e.wait_ge(sem, val)

                for op in per_eng[ename]:
                    if op.bar:
                        le, ds = bar_snap[op.idx]
                        for x, o in le.items():
                            if x != ename or ename != "pe":
                                wait(esem[x], o.sigval, ("e", x))
                        for k, v in ds.items():
                            wait(dsem[k], v, ("d", k))
                        continue
                    for d in sorted(op.deps):
                        dep = ops[d]
                        if dep.dkey is not None:
                            wait(dsem[dep.dkey], dep.dval, ("d", dep.dkey))
                        else:
                            if dep.eng == ename and (ename == "pe" or not self.same_engine_sync):
                                continue
                            wait(esem[dep.eng], dep.sigval, ("e", dep.eng))
                    ins = op.fn(e)
                    if op.dkey is not None:
                        ins.then_inc(dsem[op.dkey], 16)
                    elif op.sig:
                        ins.then_inc(esem[ename], 1)

            @block.tensor
            def _(e):
                run_engine("pe", e)

            @block.scalar
            def _(e):
                run_engine("act", e)

            @block.vector
            def _(e):
                run_engine("dve", e)

            @block.gpsimd
            def _(e):
                run_engine("pool", e)

            @block.sync
            def _(e):
                run_engine("sp", e)


class Arena:
    def __init__(self, sb, cap):
        self.sb = sb
        self.cap = cap
        self.top = 0

    def alloc(self, shape, dtype):
        n = 1
        for s in shape:
            n *= s
        nbytes = n * (4 if dtype == F32 else 2)
        off = (self.top + 63) // 64 * 64
        self.top = off + nbytes
        assert self.top <= self.cap, ("SBUF arena overflow", self.top, self.cap)
        v = self.sb[:, off // 2:(off + nbytes) // 2]
        if dtype == F32:
            v = v.bitcast(F32)
        if len(shape) == 2:
            v = v.rearrange("p (a b) -> p a b", b=shape[1])
        elif len(shape) == 3:
            v = v.rearrange("p (a b c) -> p a b c", b=shape[1], c=shape[2])
        return v


def tok_blocks(T):
    out = []
    s = 0
    while s < T:
        n = min(512, T - s)
        out.append((s, n))
        s += n
    return out


def na_blocks(p):
    if p <= 1:
        return [16, 17, 0, 1, 2, 3]
    if p == 15:
        return [12, 13, 14, 15, 16, 17]
    return [p - 2, p - 1, p, p + 1, p + 2]


TBL_IDX = [0, 1] + [2] * 12 + [3, 4]
TBL_P = [0, 1, 2, 14, 15]


def build_bias_table(rpb, hi):
    tbl = np.full((5, 8, 128, 768), NEG, dtype=np.float32)
    col = np.arange(64)
    col_start = np.clip(col - 8, 0, 48)
    for ti, p in enumerate(TBL_P):
        blks = na_blocks(p)
        for s, c in enumerate(blks):
            for kp in range(2):
                lk = 2 * c + kp
                if hi:
                    gk = lk + 32 if lk < 32 else lk - 4
                else:
                    gk = lk
                for qp in range(2):
                    lq = 2 * p + qp
                    gq = lq + 32 if hi else lq
                    rs = min(max(gq - 4, 0), 56)
                    if not (rs <= gk < rs + 8):
                        continue
                    dr = gk - gq
                    kc = col[:, None]
                    qc = col[None, :]
                    valid = (kc >= col_start[None, :]) & (kc < col_start[None, :] + 16)
                    dc = np.clip(kc - qc, -15, 15)
                    vals = rpb[:, dr + 7, :][:, dc + 15]
                    sub = np.where(valid[None], vals, np.float32(NEG))
                    tbl[ti, :, kp * 64:(kp + 1) * 64, s * 128 + qp * 64: s * 128 + (qp + 1) * 64] = sub
    return tbl


class Ctx:
    pass


DBG = {}


def build_program(stop_after=None):
    nc = bass.Bass("TRN2", target_bir_lowering=False)

    def dram(name, shape, dtype=F32, kind="ExternalInput"):
        return nc.dram_tensor(name, list(shape), dtype, kind=kind).ap()

    x_in = dram("x", [NT, D])
    mem_in = dram("mem", [NMEM, D])
    wgu1 = dram("wgu1", [NCH, 128, NK, 256])
    wd1 = dram("wd1", [NK, 128, NCH, 128])
    wgu2 = dram("wgu2", [NCH, 128, NK, 256])
    wd2 = dram("wd2", [NK, 128, NCH, 128])
    win = dram("win", [36, 128, NK, 128])
    wkv = dram("wkv", [D, 1024])
    wout = dram("wout", [D, D])
    gall_in = dram("gall", [128, 9 * 16])
    bias_in = dram("bias", [5, 8, 128, 768])
    wst_in = dram("wst", [128, 4 * 128])
    bs_in = dram("bs", [128, 4])
    lng_in = dram("lng", [128, 512])
    lnb_in = dram("lnb", [128, 512])
    ident_in = dram("ident", [128, 128])
    out = dram("out", [NOWN, D], kind="ExternalOutput")
    dbg = stop_after is not None
    ikind = "ExternalOutput" if dbg else "Internal"
    h0T = dram("h0T", [NK, 128, NT], kind=ikind)
    h1T = dram("h1T", [NK, 128, NT], kind=ikind)
    h2T = dram("h2T", [NK, 128, NOWN], kind=ikind)
    h3T = dram("h3T", [NK, 128, NOWN], kind=ikind)
    memT0 = dram("memT0", [NK, 128, NMEM], kind="Internal")
    fscr = dram("fscr", [NK, 128, 768], kind="Internal")
    ys = dram("ys", [NOWN, D], kind=ikind)

    with contextlib.ExitStack() as st:
        SB_BYTES = 206 * 1024
        sb = st.enter_context(nc.sbuf_tensor("sb", [128, SB_BYTES // 2], BF16))
        ps = st.enter_context(nc.psum_tensor("ps", [128, 4096], F32))
        P = Prog()
        A = Arena(sb, SB_BYTES)
        C = Ctx()
        C.nc, C.P, C.A, C.ps = nc, P, A, ps

        def bank(b, n=512, off=0):
            return ps[:, b * 512 + off: b * 512 + off + n]

        identf = A.alloc([128], F32)
        identb = A.alloc([128], BF16)
        onesf = A.alloc([128], F32)
        gall = A.alloc([9, 16], F32)
        stats = A.alloc([64, 8], F32)
        P.add("sp", lambda e: e.dma_start(out=identf, in_=ident_in), writes=["identf"], dkey="c0")
        P.add("sp", lambda e: e.dma_start(out=gall.rearrange("p a b -> p (a b)"), in_=gall_in), writes=["gall"], dkey="c1")
        P.add("dve", lambda e: e.tensor_copy(out=identb, in_=identf), reads=["identf"], writes=["identb"])
        P.add("dve", lambda e: e.memset(onesf, 1.0), writes=["onesf"])
        stat_ctr = [0]

        def stat():
            i = stat_ctr[0] % 64
            stat_ctr[0] += 1
            return stats[:, i, :], ("st", i)

        fTp = [A.alloc([768], F32) for _ in range(2)]
        hTtp = [A.alloc([768], F32) for _ in range(2)]
        rbcBp = A.alloc([768], F32)
        persist_top = A.top

        def to_featmajor(src, dst, dstname, ntok):
            mark = A.top
            NXB = 3
            xt = [A.alloc([D], F32) for _ in range(NXB)]
            stg = [A.alloc([NK, 128], F32) for _ in range(2)]
            ntile = ntok // 128

            def ld(j):
                xb = j % NXB
                P.add("sp", lambda e, j=j, xb=xb: e.dma_start(out=xt[xb], in_=src[j * 128:(j + 1) * 128, :]),
                      writes=[("xt", xb)], dkey=("xt", xb))

            ld(0)
            if ntile > 1:
                ld(1)
            for j in range(ntile):
                b = j % 2
                xb = j % NXB
                if j + 2 < ntile:
                    ld(j + 2)
                for k in range(NK):
                    bk = 4 * b + k // 4
                    P.add("pe", lambda e, xb=xb, k=k, bk=bk: e.transpose(
                        out=bank(bk, 128, (k % 4) * 128), in_=xt[xb][:, k * 128:(k + 1) * 128], identity=identf),
                        reads=[("xt", xb), "identf"], writes=[("ps", bk)])
                eng = "act" if b == 0 else "dve"
                if eng == "act":
                    P.add("act", lambda e, b=b: e.activation(
                        out=stg[b].rearrange("p a b -> p (a b)"), in_=ps[:, 4 * b * 512:(4 * b + 4) * 512], func=AF.Copy),
                        reads=[("ps", 4 * b + i) for i in range(4)], writes=[("stg", b)])
                else:
                    P.add("dve", lambda e, b=b: e.tensor_copy(
                        out=stg[b].rearrange("p a b -> p (a b)"), in_=ps[:, 4 * b * 512:(4 * b + 4) * 512]),
                        reads=[("ps", 4 * b + i) for i in range(4)], writes=[("stg", b)])
                P.add("sp", lambda e, j=j, b=b: e.dma_start(
                    out=dst[:, :, j * 128:(j + 1) * 128].rearrange("k p t -> p k t"), in_=stg[b]),
                    reads=[("stg", b)], writes=[(dstname, k, j) for k in range(NK)], dkey=("stg", b))
            P.barrier()
            A.top = mark

        def colnorm_stats(accv, T, factor, rbc, totbanks, tag=""):
            s1 = 1.0 / (D * factor * factor)
            s2 = EPS / (factor * factor)
            for bi, (bs, bn) in enumerate(tok_blocks(T)):
                bk = totbanks[bi]
                P.add("pe", lambda e, bk=bk, bs=bs, bn=bn: e.matmul(
                    bank(bk, bn), lhsT=onesf, rhs=accv[:, bs:bs + bn], start=True, stop=True),
                    reads=[tag + "acc", "onesf"], writes=[("ps", bk)])
                P.add("dve", lambda e, bk=bk, bs=bs, bn=bn: e.tensor_scalar(
                    out=rbc[:, bs:bs + bn], in0=bank(bk, bn), scalar1=s1, scalar2=s2, op0=ALU.mult, op1=ALU.add),
                    reads=[("ps", bk)], writes=[(tag + "rbc", bi)])
            rr = [(tag + "rbc", bi) for bi in range(len(tok_blocks(T)))]
            P.add("act", lambda e: e.activation(out=rbc[:, 0:T], in_=rbc[:, 0:T], func=AF.Sqrt), reads=rr, writes=rr)
            P.add("dve", lambda e: e.reciprocal(out=rbc[:, 0:T], in_=rbc[:, 0:T]), reads=rr, writes=rr)
            return rr

        def load_and_stats(src, srcname, tok0, T, hT, sq, accv, rbc, factor, totbanks, tag="", rname="R",
                           do_load=True, do_stats=True, acc2=None):
            tiles = range(tok0 // 128, (tok0 + T) // 128)
            if do_load:
                for q in range(4):
                    P.add("sp", lambda e, q=q: e.dma_start(
                        out=hT[:, 4 * q:4 * q + 4, 0:T],
                        in_=src[4 * q:4 * q + 4, :, tok0:tok0 + T].rearrange("k p t -> p k t")),
                        reads=[(srcname, k, j) for k in range(4 * q, 4 * q + 4) for j in tiles],
                        writes=[(rname, k) for k in range(4 * q, 4 * q + 4)], dkey=(tag + "hT", rname, q))
            if not do_stats:
                return None
            nsq = len(sq)
            for k in range(NK):
                if k == 0:
                    P.add("act", lambda e: e.activation(out=accv[:, 0:T], in_=hT[:, 0, 0:T], func=AF.Square),
                          reads=[(rname, 0)], writes=[tag + "acc"])
                elif k == 1 and acc2 is not None:
                    P.add("act", lambda e: e.activation(out=acc2[:, 0:T], in_=hT[:, 1, 0:T], func=AF.Square),
                          reads=[(rname, 1)], writes=[tag + "acc2"])
                else:
                    sbi = k % nsq
                    if acc2 is None or k % 2 == 0:
                        tgt, tname = accv, tag + "acc"
                    else:
                        tgt, tname = acc2, tag + "acc2"
                    P.add("act", lambda e, k=k, sbi=sbi: e.activation(out=sq[sbi][:, 0:T], in_=hT[:, k, 0:T], func=AF.Square),
                          reads=[(rname, k)], writes=[(tag + "sq", sbi)])
                    P.add("dve", lambda e, sbi=sbi, tgt=tgt: e.tensor_tensor(
                        out=tgt[:, 0:T], in0=tgt[:, 0:T], in1=sq[sbi][:, 0:T], op=ALU.add),
                        reads=[tname, (tag + "sq", sbi)], writes=[tname])
            if acc2 is not None:
                P.add("dve", lambda e: e.tensor_tensor(
                    out=accv[:, 0:T], in0=accv[:, 0:T], in1=acc2[:, 0:T], op=ALU.add),
                    reads=[tag + "acc", tag + "acc2"], writes=[tag + "acc"])
            return colnorm_stats(accv, T, factor, rbc, totbanks, tag)

        def prenorm_T(src, srcname, tok0, T, gi, Rbuf, dst, dstres, tmp, tag=""):
            hT = Rbuf
            sq, accv, rbc = tmp[0], tmp[1], tmp[2]
            acc2_ = tmp[3] if len(tmp) > 3 else None
            rr = load_and_stats(src, srcname, tok0, T, hT, sq, accv, rbc, 1.0, (6, 7), tag, acc2=acc2_)
            for k in range(NK):
                P.add("dve", lambda e, k=k: e.scalar_tensor_tensor(
                    out=dst[:, k, 0:T], in0=hT[:, k, 0:T], scalar=gall[:, gi, k:k + 1], in1=rbc[:, 0:T],
                    op0=ALU.mult, op1=ALU.mult),
                    reads=[("R", k), "gall"] + rr, writes=[dstres(k)])

        def proj_postnorm(T, nc_in, lhs_fn, rhs_fn, rd_fn, wload_fn, src, srcname, tok0, dst, dstname,
                          gi_post, factor, tmp, defer=False, hook=None, fT2=None):
            sq, accv, rbc, fT, hTt = tmp
            tag = "B"
            f2 = fT2 if fT2 is not None else fT
            f2n = "fT2" if fT2 is not None else "fT"
            blocks = tok_blocks(T)
            tiles = range(tok0 // 128, (tok0 + T) // 128)
            for dc in range(NK):
                s = dc % 2
                if wload_fn is not None:
                    wload_fn(dc)
                for bi, (bs, bn) in enumerate(blocks):
                    bk = 2 * s + bi
                    for c in range(nc_in):
                        P.add("pe", lambda e, dc=dc, c=c, bk=bk, bs=bs, bn=bn: e.matmul(
                            bank(bk, bn), lhsT=lhs_fn(dc, c), rhs=rhs_fn(c, bs, bn),
                            start=(c == 0), stop=(c == nc_in - 1)),
                            reads=rd_fn(dc, c, bi), writes=[("ps", bk)])
                if hook is not None:
                    hook(dc)
                for bi, (bs, bn) in enumerate(blocks):
                    bk = 2 * s + bi
                    P.add("act", lambda e, s=s, bk=bk, bs=bs, bn=bn: e.activation(
                        out=fT[s][:, bs:bs + bn], in_=bank(bk, bn), func=AF.Copy),
                        reads=[("ps", bk)], writes=[("fT", s, bi)])
                fr = [("fT", s, bi) for bi in range(len(blocks))]
                P.add("sp", lambda e, dc=dc, s=s: e.dma_start(out=fscr[dc, :, 0:T], in_=fT[s][:, 0:T]),
                      reads=fr, writes=[("fscr", dc)], dkey=("fT", s))
                if dc == 0:
                    P.add("act", lambda e, s=s: e.activation(out=accv[:, 0:T], in_=fT[s][:, 0:T], func=AF.Square),
                          reads=fr, writes=[tag + "acc"])
                else:
                    P.add("act", lambda e, s=s: e.activation(out=sq[s][:, 0:T], in_=fT[s][:, 0:T], func=AF.Square),
                          reads=fr, writes=[(tag + "sq", s)])
                    P.add("dve", lambda e, s=s: e.tensor_tensor(
                        out=accv[:, 0:T], in0=accv[:, 0:T], in1=sq[s][:, 0:T], op=ALU.add),
                        reads=[tag + "acc", (tag + "sq", s)], writes=[tag + "acc"])
            rr = colnorm_stats(accv, T, factor, rbc, (4, 5), tag)

            def loads(dc):
                s = dc % 2
                fw_ = [(f2n, s, bi) for bi in range(len(blocks))]
                P.add("sp", lambda e, dc=dc, s=s: e.dma_start(out=f2[s][:, 0:T], in_=fscr[dc, :, 0:T]),
                      reads=[("fscr", dc)], writes=fw_, dkey=(f2n, s))
                P.add("sp", lambda e, dc=dc, s=s: e.dma_start(out=hTt[s][:, 0:T], in_=src[dc, :, tok0:tok0 + T]),
                      reads=[(srcname, dc, j) for j in tiles], writes=[("hTt", s)], dkey=("hTt", s))

            def step(dc):
                s = dc % 2
                fw_ = [(f2n, s, bi) for bi in range(len(blocks))]
                if dc == 0:
                    loads(0)
                P.add("dve", lambda e, dc=dc, s=s: e.scalar_tensor_tensor(
                    out=f2[s][:, 0:T], in0=f2[s][:, 0:T], scalar=gall[:, gi_post, dc:dc + 1], in1=rbc[:, 0:T],
                    op0=ALU.mult, op1=ALU.mult),
                    reads=fw_ + rr + ["gall"], writes=fw_)
                if dc + 1 < NK:
                    loads(dc + 1)
                P.add("dve", lambda e, s=s: e.tensor_tensor(
                    out=hTt[s][:, 0:T], in0=f2[s][:, 0:T], in1=hTt[s][:, 0:T], op=ALU.add),
                    reads=fw_ + [("hTt", s)], writes=[("hTt", s)])
                P.add("sp", lambda e, dc=dc, s=s: e.dma_start(out=dst[dc, :, tok0:tok0 + T], in_=hTt[s][:, 0:T]),
                      reads=[("hTt", s)], writes=[(dstname, dc, j) for j in tiles], dkey=("hTt", s))

            steps = [(lambda dc=dc: step(dc)) for dc in range(NK)]
            if defer:
                return steps
            for st_ in steps:
                st_()
            return []

        def ffn_phase(src, srcname, dst, dstname, passes, wgu, wd, gi_pre, gi_post, dbg_stop=None, pre_pending=None):
            mark = A.top
            TM = 768
            Rb = A.alloc([NCH * TM], BF16)
            Rf = Rb.bitcast(F32)
            hT = Rf[:, 0:NK * TM].rearrange("p (k t) -> p k t", t=TM)
            aT = Rb.rearrange("p (c t) -> p c t", t=TM)
            xnT = A.alloc([NK, TM], BF16)
            wgub = [A.alloc([NK, 256], BF16) for _ in range(3)]
            wdb = [A.alloc([NCH, 128], BF16) for _ in range(2)]
            stmp = [A.alloc([TM], F32) for _ in range(2)]
            sqA = [A.alloc([TM], F32) for _ in range(4)]
            accA = A.alloc([TM], F32)
            accA2 = A.alloc([TM], F32)
            rbcA = A.alloc([TM], F32)
            sqB = [A.alloc([TM], F32) for _ in range(2)]
            accB = A.alloc([TM], F32)
            rbcB = rbcBp
            fT = [A.alloc([TM], F32) for _ in range(2)]
            hTt = hTtp
            wctr = [0]

            def ares(i):
                return ("R", i // 2) if i < 32 else ("aTx", i)

            def do_prenorm(tok0, T):
                prenorm_T(src, srcname, tok0, T, gi_pre, hT, xnT, lambda k: ("xnT", k), (sqA, accA, rbcA, accA2), "A")

            pending = list(pre_pending or [])
            do_prenorm(*passes[0])
            for pi, (tok0, T) in enumerate(passes):
                blocks = tok_blocks(T)
                for i in range(NCH):
                    s = wctr[0] % 3
                    wctr[0] += 1
                    P.add("pool", lambda e, i=i, s=s: e.dma_start(out=wgub[s], in_=wgu[i]),
                          writes=[("wgu", s)], dkey=("wgu", s))
                    pset = i % 2
                    for half in range(2):
                        for bi, (bs, bn) in enumerate(blocks):
                            bk = 4 * pset + 2 * half + bi
                            for k in range(NK):
                                P.add("pe", lambda e, s=s, half=half, k=k, bk=bk, bs=bs, bn=bn: e.matmul(
                                    bank(bk, bn), lhsT=wgub[s][:, k, half * 128:(half + 1) * 128],
                                    rhs=xnT[:, k, bs:bs + bn], start=(k == 0), stop=(k == NK - 1)),
                                    reads=[("wgu", s), ("xnT", k)], writes=[("ps", bk)])
                    for bi, (bs, bn) in enumerate(blocks):
                        gb = 4 * pset + bi
                        ub = 4 * pset + 2 + bi
                        P.add("act", lambda e, pset=pset, gb=gb, bs=bs, bn=bn: e.activation(
                            out=stmp[pset][:, bs:bs + bn], in_=bank(gb, bn), func=AF.Silu),
                            reads=[("ps", gb)], writes=[("stmp", pset, bi)])
                        P.add("dve", lambda e, i=i, pset=pset, ub=ub, bs=bs, bn=bn: e.tensor_tensor(
                            out=aT[:, i, bs:bs + bn], in0=bank(ub, bn), in1=stmp[pset][:, bs:bs + bn], op=ALU.mult),
                            reads=[("ps", ub), ("stmp", pset, bi)], writes=[ares(i)])
                    if pending and i >= 3 and i % 2 == 1:
                        pending.pop(0)()
                while pending:
                    pending.pop(0)()

                def wload(dc):
                    s = dc % 2
                    P.add("pool", lambda e, dc=dc, s=s: e.dma_start(out=wdb[s], in_=wd[dc]),
                          writes=[("wd", s)], dkey=("wd", s))

                pending = proj_postnorm(T, NCH,
                                        lambda dc, c: wdb[dc % 2][:, c, :],
                                        lambda c, bs, bn: aT[:, c, bs:bs + bn],
                                        lambda dc, c, bi: [("wd", dc % 2), ares(c)],
                                        wload, src, srcname, tok0, dst, dstname, gi_post, 0.5,
                                        (sqB, accB, rbcB, fT, hTt), defer=True, fT2=fTp)
                if pi + 1 < len(passes):
                    do_prenorm(*passes[pi + 1])
            P.barrier()
            A.top = mark
            return pending

        def final_phase(src, srcname, pending=None):
            mark = A.top
            TM = 512
            hTf = [A.alloc([NK, TM], F32) for _ in range(2)]
            sqs = [[A.alloc([TM], F32) for _ in range(4)] for _ in range(2)]
            accs = [A.alloc([TM], F32) for _ in range(2)]
            accs2 = [A.alloc([TM], F32) for _ in range(2)]
            rbcs = [A.alloc([TM], F32) for _ in range(2)]
            ot = [A.alloc([D], F32) for _ in range(2)]
            blks = [(0, 512), (512, 512), (1024, 512), (1536, 512)]

            def fload(i):
                tok0, T = blks[i]
                load_and_stats(src, srcname, tok0, T, hTf[i % 2], None, None, None, 1.0, (6, 7), "F%d" % (i % 2),
                               ("RF", i % 2), do_load=True, do_stats=False)

            def fstats(i):
                tok0, T = blks[i]
                q = i % 2
                hT = hTf[q]
                rn = ("RF", q)
                rr = load_and_stats(src, srcname, tok0, T, hT, sqs[q], accs[q], rbcs[q], 1.0, (6, 7), "F%d" % q, rn,
                                    do_load=False, do_stats=True, acc2=accs2[q])
                for k in range(NK):
                    P.add("dve", lambda e, k=k, hT=hT, q=q: e.scalar_tensor_tensor(
                        out=hT[:, k, 0:T], in0=hT[:, k, 0:T], scalar=gall[:, G_FINAL, k:k + 1], in1=rbcs[q][:, 0:T],
                        op0=ALU.mult, op1=ALU.mult),
                        reads=[(rn, k), "gall"] + rr, writes=[(rn, k)])

            fload(0)
            fstats(0)
            jc = 0
            pending = list(pending or [])
            for i, (tok0, T) in enumerate(blks):
                hT = hTf[i % 2]
                rn = ("RF", i % 2)
                if i == 1:
                    while pending:
                        pending.pop(0)()
                if i + 1 < len(blks):
                    fload(i + 1)
                    fstats(i + 1)
                for j in range(T // 128):
                    b = jc % 2
                    jc += 1
                    for k in range(NK):
                        bk = 4 * b + k // 4
                        P.add("pe", lambda e, j=j, k=k, bk=bk, hT=hT: e.transpose(
                            out=bank(bk, 128, (k % 4) * 128), in_=hT[:, k, j * 128:(j + 1) * 128], identity=identf),
                            reads=[(rn, k), "identf"], writes=[("ps", bk)])
                    if b == 0:
                        P.add("act", lambda e, b=b: e.activation(
                            out=ot[b], in_=ps[:, 4 * b * 512:(4 * b + 4) * 512], func=AF.Copy),
                            reads=[("ps", 4 * b + i2) for i2 in range(4)], writes=[("ot", b)])
                    else:
                        P.add("dve", lambda e, b=b: e.tensor_copy(out=ot[b], in_=ps[:, 4 * b * 512:(4 * b + 4) * 512]),
                              reads=[("ps", 4 * b + i2) for i2 in range(4)], writes=[("ot", b)])
                    P.add("sp", lambda e, j=j, b=b, tok0=tok0: e.dma_start(
                        out=out[tok0 + j * 128: tok0 + (j + 1) * 128, :], in_=ot[b]),
                        reads=[("ot", b)], writes=[("out", tok0 // 128 + j)], dkey=("ot", b))
            P.barrier()
            A.top = mark

        def mixer_phase(pending=None):
            mark0 = A.top
            TM = 768
            xn2T = A.alloc([NK, NT], BF16)
            mark = A.top
            hT = A.alloc([NK, TM], F32)
            sq = [A.alloc([TM], F32) for _ in range(2)]
            accv = A.alloc([TM], F32)
            rbc = A.alloc([TM], F32)
            memT = A.alloc([NK, NMEM], BF16)
            pending = list(pending or [])
            for bi3 in range(3):
                tok0 = bi3 * 768
                if bi3 == 2:
                    while pending:
                        pending.pop(0)()
                prenorm_T(h1T, "h1T", tok0, 768, G_MIXPRE, hT, xn2T[:, :, tok0:tok0 + 768],
                          lambda k, bi3=bi3: ("xn2T", k, bi3), (sq, accv, rbc))
                for _ in range(8):
                    if pending:
                        pending.pop(0)()
            prenorm_T(memT0, "memT0", 0, NMEM, G_MEM, hT, memT, lambda k: ("memT", k), (sq, accv, rbc))
            P.barrier()
            if DBG.get("stop") == "m0":
                return
            wkvb = A.alloc([NK, 1024], BF16)
            for k in range(NK):
                P.add("pool", lambda e, k=k: e.dma_start(out=wkvb[:, k, :], in_=wkv[k * 128:(k + 1) * 128, :]),
                      writes=[("wkvb", k)], dkey=("wkvb", k))
            A_keep = A.top
            A.top = mark
            A.top = A_keep
            kmT = A.alloc([4, NMEM], BF16)
            vm = A.alloc([2, 4, 130], BF16)
            P.add("dve", lambda e: e.memset(vm.rearrange("p a b c -> p (a b c)"), 1.0), writes=["vm"])
            for h in range(4):
                bk = h % 2
                for k in range(NK):
                    P.add("pe", lambda e, h=h, k=k, bk=bk: e.matmul(
                        bank(bk, NMEM), lhsT=wkvb[:, k, h * 128:(h + 1) * 128], rhs=memT[:, k, :],
                        start=(k == 0), stop=(k == NK - 1)),
                        reads=[("wkvb", k), ("memT", k)], writes=[("ps", bk)])
                P.add("act", lambda e, h=h, bk=bk: e.activation(out=kmT[:, h, :], in_=bank(bk, NMEM), func=AF.Copy),
                      reads=[("ps", bk)], writes=[("kmT", h)])
            for mt in range(2):
                bk = 2 + mt
                for k in range(NK):
                    P.add("pe", lambda e, mt=mt, k=k, bk=bk: e.matmul(
                        bank(bk, 512), lhsT=memT[:, k, mt * 128:(mt + 1) * 128], rhs=wkvb[:, k, 512:1024],
                        start=(k == 0), stop=(k == NK - 1)),
                        reads=[("wkvb", k), ("memT", k)], writes=[("ps", bk)])
                P.add("dve", lambda e, mt=mt, bk=bk: e.tensor_copy(
                    out=vm[:, mt, :, 0:128], in_=bank(bk, 512).rearrange("p (h d) -> p h d", d=128)),
                    reads=[("ps", bk), "vm"], writes=["vm"])
            P.barrier()
            if DBG.get("stop") == "m1":
                return
            Bsub = Arena(sb, A_keep)
            Bsub.top = mark
            def balloc(shape, dt):
                try:
                    return Bsub.alloc(shape, dt)
                except AssertionError:
                    return A.alloc(shape, dt)

            winb = [balloc([NK, 128], BF16) for _ in range(4)]
            qT = [balloc([NOWN], BF16) for _ in range(2)]
            kT = [balloc([NT], BF16) for _ in range(2)]
            vT = balloc([NT], BF16)
            V = [balloc([18, 130], BF16) for _ in range(2)]
            btile = [balloc([768], F32) for _ in range(2)]
            sbx = [balloc([768], F32) for _ in range(2)]
            PT = [balloc([768], BF16) for _ in range(2)]
            yh = [balloc([16, 128], F32) for _ in range(2)]
            PTm = [balloc([2, 512], BF16) for _ in range(2)]
            osb = [balloc([130], F32) for _ in range(2)]
            for b in range(2):
                P.add("dve", lambda e, b=b: e.memset(V[b].rearrange("p a b -> p (a b)"), 1.0), writes=[("V", b)])
            wctr = [0]
            ectr = [0]

            def projT(group, ntok, scale, dstv, dstres):
                s = wctr[0] % 4
                wctr[0] += 1
                P.add("pool", lambda e, s=s: e.dma_start(out=winb[s], in_=win[group]),
                      writes=[("winb", s)], dkey=("winb", s))
                for bi, (bs, bn) in enumerate(tok_blocks(ntok)):
                    bk = ectr[0] % 2
                    ectr[0] += 1
                    for k in range(NK):
                        P.add("pe", lambda e, s=s, k=k, bk=bk, bs=bs, bn=bn: e.matmul(
                            bank(bk, bn), lhsT=winb[s][:, k, :], rhs=xn2T[:, k, bs:bs + bn],
                            start=(k == 0), stop=(k == NK - 1)),
                            reads=[("winb", s)] + [("xn2T", k, b3) for b3 in range(3)], writes=[("ps", bk)])
                    if bk == 0:
                        P.add("act", lambda e, bk=bk, bs=bs, bn=bn: e.activation(
                            out=dstv[:, bs:bs + bn], in_=bank(bk, bn), func=AF.Copy, scale=float(scale)),
                            reads=[("ps", bk)], writes=[(dstres, bi)])
                    else:
                        P.add("dve", lambda e, bk=bk, bs=bs, bn=bn: e.tensor_scalar(
                            out=dstv[:, bs:bs + bn], in0=bank(bk, bn), scalar1=float(scale), scalar2=None, op0=ALU.mult),
                            reads=[("ps", bk)], writes=[(dstres, bi)])
                return [(dstres, bi) for bi in range(len(tok_blocks(ntok)))]

            SC = 128.0 ** -0.5
            hctr = [0]

            def finish_head(yhb, colbase):
                b = yhb
                P.add("sp", lambda e, b=b, colbase=colbase: e.dma_start(
                    out=ys[:, colbase:colbase + 128].rearrange("(t p) c -> p t c", p=128), in_=yh[b]),
                    reads=[("yh", b)], writes=[("ys", colbase)], dkey=("yh", b))

            for hm in range(4):
                hb = hctr[0] % 2
                hctr[0] += 1
                qres = projT(32 + hm, NOWN, SC, qT[hb], ("qT", hb))
                if DBG.get("cut") == "a":
                    P.barrier()
                    return
                for tb in range(4):
                    pb = tb % 2
                    for j in range(2):
                        bk = 4 + 2 * pb + j
                        P.add("pe", lambda e, hm=hm, hb=hb, tb=tb, j=j, bk=bk: e.matmul(
                            bank(bk, 512), lhsT=kmT[:, hm, j * 128:(j + 1) * 128], rhs=qT[hb][:, tb * 512:(tb + 1) * 512],
                            start=True, stop=True),
                            reads=[("kmT", hm), (("qT", hb), tb)], writes=[("ps", bk)])
                        P.add("act", lambda e, pb=pb, j=j, bk=bk: e.activation(
                            out=PTm[pb][:, j, :], in_=bank(bk, 512), func=AF.Exp),
                            reads=[("ps", bk)], writes=[("PTm", pb, j)])
                    if DBG.get("cut") == "b":
                        P.barrier()
                        return
                    for tt in range(4):
                        tile = tb * 4 + tt
                        ob = tile % 2
                        for j in range(2):
                            P.add("pe", lambda e, hm=hm, pb=pb, tt=tt, j=j, ob=ob: e.matmul(
                                bank(2 + ob, 130), lhsT=PTm[pb][:, j, tt * 128:(tt + 1) * 128], rhs=vm[:, j, hm, 0:130],
                                start=(j == 0), stop=(j == 1)),
                                reads=[("PTm", pb, j), "vm"], writes=[("ps", 2 + ob)])
                        rc, rcres = stat()
                        P.add("act", lambda e, ob=ob: e.activation(out=osb[ob], in_=bank(2 + ob, 130), func=AF.Copy),
                              reads=[("ps", 2 + ob)], writes=[("osb", ob)])
                        P.add("dve", lambda e, ob=ob, rc=rc: e.reciprocal(out=rc[:, 0:1], in_=osb[ob][:, 128:129]),
                              reads=[("osb", ob)], writes=[rcres])
                        P.add("dve", lambda e, ob=ob, rc=rc, hb=hb, tile=tile: e.tensor_scalar(
                            out=yh[hb][:, tile, :], in0=osb[ob][:, 0:128], scalar1=rc[:, 0:1], scalar2=None, op0=ALU.mult),
                            reads=[("osb", ob), rcres], writes=[("yh", hb)])
                if DBG.get("cut") == "c":
                    P.barrier()
                    return
                finish_head(hb, 1536 + hm * 128)
                if DBG.get("cut") == "d":
                    P.barrier()
                    return

            if DBG.get("stop") == "m2":
                P.barrier()
                return
            for h in range(DBG.get("nheads", 8)):
                hb = hctr[0] % 2
                hctr[0] += 1
                qres = projT(h, NOWN, SC, qT[hb], ("qT", hb))
                kres = projT(8 + h, NT, 1.0, kT[hb], ("kT", hb))
                vres = projT(16 + h, NT, 1.0, vT, "vT")
                for c0 in range(0, 18, 8):
                    bk = ectr[0] % 2
                    ectr[0] += 1
                    n8 = min(8, 18 - c0)
                    for c in range(c0, c0 + n8):
                        slot = c - c0
                        P.add("pe", lambda e, c=c, slot=slot, bk=bk: e.transpose(
                            out=bank(bk).bitcast(BF16)[:, slot * 128:(slot + 1) * 128], in_=vT[:, c * 128:(c + 1) * 128],
                            identity=identb),
                            reads=vres + ["identb"], writes=[("ps", bk)])
                    P.add("dve", lambda e, c0=c0, n8=n8, bk=bk, hb=hb: e.tensor_copy(
                        out=V[hb][:, c0:c0 + n8, 0:128],
                        in_=bank(bk).bitcast(BF16)[:, 0:n8 * 128].rearrange("p (c d) -> p c d", d=128)),
                        reads=[("ps", bk)], writes=[("V", hb)])
                def na_scores(p):
                    blks = na_blocks(p)
                    ub = p % 2
                    P.add("sp", lambda e, p=p, h=h, ub=ub: e.dma_start(out=btile[ub], in_=bias_in[TBL_IDX[p], h]),
                          writes=[("btile", ub)], dkey=("btile", ub))
                    for s, c in enumerate(blks):
                        bk = 4 + 2 * ub + s // 4
                        P.add("pe", lambda e, hb=hb, p=p, s=s, c=c, bk=bk: e.matmul(
                            bank(bk, 128, (s % 4) * 128), lhsT=kT[hb][:, c * 128:(c + 1) * 128],
                            rhs=qT[hb][:, p * 128:(p + 1) * 128], start=True, stop=True),
                            reads=kres + qres, writes=[("ps", bk)])

                na_scores(0)
                for p in range(16):
                    blks = na_blocks(p)
                    nb = len(blks)
                    ub = p % 2
                    if p + 1 < 16:
                        na_scores(p + 1)
                    P.add("dve", lambda e, ub=ub, nb=nb: e.tensor_tensor(
                        out=sbx[ub][:, 0:nb * 128], in0=ps[:, (4 + 2 * ub) * 512:(4 + 2 * ub) * 512 + nb * 128],
                        in1=btile[ub][:, 0:nb * 128], op=ALU.add),
                        reads=[("ps", 4 + 2 * ub), ("ps", 5 + 2 * ub), ("btile", ub)], writes=[("sbx", ub)])
                    P.add("act", lambda e, ub=ub, nb=nb: e.activation(
                        out=PT[ub][:, 0:nb * 128], in_=sbx[ub][:, 0:nb * 128], func=AF.Exp),
                        reads=[("sbx", ub)], writes=[("PT", ub)])
                    for s, c in enumerate(blks):
                        P.add("pe", lambda e, ub=ub, s=s, c=c, hb=hb, nb=nb: e.matmul(
                            bank(2 + ub, 130), lhsT=PT[ub][:, s * 128:(s + 1) * 128], rhs=V[hb][:, c, 0:130],
                            start=(s == 0), stop=(s == nb - 1)),
                            reads=[("PT", ub), ("V", hb)], writes=[("ps", 2 + ub)])
                    rc, rcres = stat()
                    P.add("act", lambda e, ub=ub: e.activation(out=osb[ub], in_=bank(2 + ub, 130), func=AF.Copy),
                          reads=[("ps", 2 + ub)], writes=[("osb", ub)])
                    P.add("dve", lambda e, ub=ub, rc=rc: e.reciprocal(out=rc[:, 0:1], in_=osb[ub][:, 128:129]),
                          reads=[("osb", ub)], writes=[rcres])
                    P.add("dve", lambda e, ub=ub, rc=rc, hb=hb, p=p: e.tensor_scalar(
                        out=yh[hb][:, p, :], in0=osb[ub][:, 0:128], scalar1=rc[:, 0:1], scalar2=None, op0=ALU.mult),
                        reads=[("osb", ub), rcres], writes=[("yh", hb)])
                finish_head(hb, h * 128)
            P.barrier()
            if DBG.get("stop") == "m3":
                return

            Bsub.top = mark
            A.top = A_keep
            wz = balloc([NK, 1024], BF16)
            wsTf = balloc([4, 128], F32)
            wsTb = balloc([4, 128], BF16)
            bsv = balloc([4], F32)
            lng = balloc([512], F32)
            lnb = balloc([512], F32)
            t1 = [balloc([1024], F32) for _ in range(2)]
            ge = [balloc([1024], F32) for _ in range(2)]
            vln = [balloc([512], BF16) for _ in range(2)]
            ysg = [balloc([512], F32) for _ in range(2)]
            for g8 in range(8):
                P.add("pool", lambda e, g8=g8: e.dma_start(out=wz[:, :, g8 * 128:(g8 + 1) * 128], in_=win[24 + g8]),
                      writes=[("wz", g8)], dkey=("wz", g8))
            P.add("sp", lambda e: e.dma_start(out=wsTf.rearrange("p a b -> p (a b)"), in_=wst_in), writes=["wsTf"], dkey="sg0")
            P.add("sp", lambda e: e.dma_start(out=bsv, in_=bs_in), writes=["bsv"], dkey="sg1")
            P.add("sp", lambda e: e.dma_start(out=lng, in_=lng_in), writes=["lng"], dkey="sg2")
            P.add("sp", lambda e: e.dma_start(out=lnb, in_=lnb_in), writes=["lnb"], dkey="sg3")
            P.add("dve", lambda e: e.tensor_copy(out=wsTb.rearrange("p a b -> p (a b)"), in_=wsTf.rearrange("p a b -> p (a b)")),
                  reads=["wsTf"], writes=["wsTb"])
            def zmm(t):
                zb = 2 * (t % 2)
                for half in range(2):
                    for k in range(NK):
                        P.add("pe", lambda e, t=t, k=k, half=half, zb=zb: e.matmul(
                            bank(zb + half, 512), lhsT=xn2T[:, k, t * 128:(t + 1) * 128], rhs=wz[:, k, half * 512:(half + 1) * 512],
                            start=(k == 0), stop=(k == NK - 1)),
                            reads=[("wz", g8) for g8 in range(4 * half, 4 * half + 4)] + [("xn2T", k, b3) for b3 in range(3)],
                            writes=[("ps", zb + half)])

            zmm(0)
            for t in range(16):
                b = t % 2
                zb = 2 * b
                if t + 1 < 16:
                    zmm(t + 1)
                zz = ps[:, zb * 512:(zb + 2) * 512]
                zr = [("ps", zb), ("ps", zb + 1)]
                P.add("act", lambda e, b=b, zz=zz: e.activation(out=t1[b], in_=zz, func=AF.Square), reads=zr, writes=[("t1", b)])
                P.add("dve", lambda e, b=b: e.tensor_scalar(out=t1[b], in0=t1[b], scalar1=0.044715, scalar2=1.0,
                                                            op0=ALU.mult, op1=ALU.add), reads=[("t1", b)], writes=[("t1", b)])
                P.add("dve", lambda e, b=b, zz=zz: e.tensor_tensor(out=t1[b], in0=zz, in1=t1[b], op=ALU.mult),
                      reads=zr + [("t1", b)], writes=[("t1", b)])
                P.add("act", lambda e, b=b: e.activation(out=t1[b], in_=t1[b], func=AF.Sigmoid, scale=1.5957691216),
                      reads=[("t1", b)], writes=[("t1", b)])
                P.add("dve", lambda e, b=b, zz=zz: e.tensor_tensor(out=ge[b], in0=zz, in1=t1[b], op=ALU.mult),
                      reads=zr + [("t1", b)], writes=[("ge", b)])
                sm, smr = stat()
                s2, s2r = stat()
                for g in range(4):
                    P.add("dve", lambda e, b=b, g=g, sm=sm: e.tensor_scalar(
                        out=t1[b][:, g * 128:(g + 1) * 128], in0=ge[b][:, 512 + g * 128:512 + (g + 1) * 128],
                        scalar1=1.0, scalar2=None, op0=ALU.mult, op1=ALU.add, accum_out=sm[:, g:g + 1]),
                        reads=[("ge", b), ("t1", b)], writes=[("t1", b), smr])
                P.add("dve", lambda e, sm=sm: e.tensor_scalar(out=sm[:, 0:4], in0=sm[:, 0:4], scalar1=1.0 / 128, scalar2=None,
                                                              op0=ALU.mult), reads=[smr], writes=[smr])
                for g in range(4):
                    P.add("dve", lambda e, b=b, g=g, sm=sm: e.tensor_scalar(
                        out=t1[b][:, g * 128:(g + 1) * 128], in0=ge[b][:, 512 + g * 128:512 + (g + 1) * 128],
                        scalar1=sm[:, g:g + 1], scalar2=None, op0=ALU.subtract),
                        reads=[("ge", b), smr, ("t1", b)], writes=[("t1", b)])
                    P.add("act", lambda e, b=b, g=g, s2=s2: e.activation(
                        out=t1[b][:, 512 + g * 128:512 + (g + 1) * 128], in_=t1[b][:, g * 128:(g + 1) * 128],
                        func=AF.Square, accum_out=s2[:, g:g + 1]),
                        reads=[("t1", b)], writes=[("t1", b), s2r])
                P.add("dve", lambda e, s2=s2: e.tensor_scalar(out=s2[:, 0:4], in0=s2[:, 0:4], scalar1=1.0 / 128, scalar2=EPS,
                                                              op0=ALU.mult, op1=ALU.add), reads=[s2r], writes=[s2r])
                P.add("act", lambda e, s2=s2: e.activation(out=s2[:, 0:4], in_=s2[:, 0:4], func=AF.Sqrt), reads=[s2r], writes=[s2r])
                P.add("dve", lambda e, s2=s2: e.reciprocal(out=s2[:, 0:4], in_=s2[:, 0:4]), reads=[s2r], writes=[s2r])
                for g in range(4):
                    P.add("dve", lambda e, b=b, g=g, s2=s2: e.scalar_tensor_tensor(
                        out=t1[b][:, g * 128:(g + 1) * 128], in0=t1[b][:, g * 128:(g + 1) * 128], scalar=s2[:, g:g + 1],
                        in1=lng[:, g * 128:(g + 1) * 128], op0=ALU.mult, op1=ALU.mult),
                        reads=[("t1", b), s2r, "lng"], writes=[("t1", b)])
                P.add("dve", lambda e, b=b: e.tensor_tensor(out=vln[b], in0=t1[b][:, 0:512], in1=lnb, op=ALU.add),
                      reads=[("t1", b), "lnb"], writes=[("vln", b)])
                mb = 4 + b
                for g in range(4):
                    P.add("pe", lambda e, b=b, g=g, mb=mb: e.matmul(
                        bank(mb, 128, g * 128), lhsT=wsTb[:, g, :], rhs=vln[b][:, g * 128:(g + 1) * 128], start=True, stop=True),
                        reads=["wsTb", ("vln", b)], writes=[("ps", mb)])
                for g in range(4):
                    P.add("dve", lambda e, b=b, g=g, mb=mb: e.scalar_tensor_tensor(
                        out=ysg[b][:, g * 128:(g + 1) * 128], in0=bank(mb, 128, g * 128), scalar=bsv[:, g:g + 1],
                        in1=ge[b][:, g * 128:(g + 1) * 128], op0=ALU.add, op1=ALU.mult),
                        reads=[("ps", mb), "bsv", ("ge", b)], writes=[("ysg", b)])
                P.add("sp", lambda e, t=t, b=b: e.dma_start(out=ys[t * 128:(t + 1) * 128, 1024:1536], in_=ysg[b]),
                      reads=[("ysg", b)], writes=[("ys_sg", t)], dkey=("ysg", b))
            P.barrier()
            A.top = mark0

        def wout_phase():
            mark = A.top
            TB = 512
            NB_ = NOWN // TB
            woutb = A.alloc([NK, D], BF16)
            yt = [A.alloc([D], F32) for _ in range(2)]
            ysb = [A.alloc([D], BF16) for _ in range(2)]
            yT = [A.alloc([NK, TB], BF16) for _ in range(2)]
            sq = [A.alloc([TB], F32) for _ in range(2)]
            accv = A.alloc([TB], F32)
            rbc = rbcBp
            fT = [A.alloc([TB], F32) for _ in range(2)]
            fT2 = fTp
            hTt = hTtp
            for k in range(NK):
                P.add("pool", lambda e, k=k: e.dma_start(out=woutb[:, k, :], in_=wout[k * 128:(k + 1) * 128, :]),
                      writes=[("woutb", k)], dkey=("woutb", k))
            secs = [(0, 1024), (1024, 512), (1536, 512)]

            def front(blk, tt):
                t = blk * 4 + tt
                b = t % 2
                P.add("sp", lambda e, t=t, b=b: e.dma_start(out=yt[b], in_=ys[t * 128:(t + 1) * 128, :]),
                      writes=[("yt", b)], dkey=("yt", b))
                ss, ssr = stat()
                for si, (s0, sn) in enumerate(secs):
                    P.add("act", lambda e, b=b, si=si, s0=s0, sn=sn, ss=ss: e.activation(
                        out=ysb[b][:, s0:s0 + sn], in_=yt[b][:, s0:s0 + sn], func=AF.Square, accum_out=ss[:, si:si + 1]),
                        reads=[("yt", b)], writes=[("ysb", b), ssr])
                    P.add("dve", lambda e, si=si, sn=sn, ss=ss: e.tensor_scalar(
                        out=ss[:, si:si + 1], in0=ss[:, si:si + 1], scalar1=1.0 / sn, scalar2=EPS, op0=ALU.mult, op1=ALU.add),
                        reads=[ssr], writes=[ssr])
                P.add("act", lambda e, ss=ss: e.activation(out=ss[:, 0:3], in_=ss[:, 0:3], func=AF.Sqrt), reads=[ssr], writes=[ssr])
                P.add("dve", lambda e, ss=ss: e.reciprocal(out=ss[:, 0:3], in_=ss[:, 0:3]), reads=[ssr], writes=[ssr])
                for si, (s0, sn) in enumerate(secs):
                    P.add("dve", lambda e, b=b, si=si, s0=s0, sn=sn, ss=ss: e.tensor_scalar(
                        out=ysb[b][:, s0:s0 + sn], in0=yt[b][:, s0:s0 + sn], scalar1=ss[:, si:si + 1], scalar2=None, op0=ALU.mult),
                        reads=[("yt", b), ssr], writes=[("ysb", b)])

            def back(blk, tt):
                t = blk * 4 + tt
                b = t % 2
                yb = blk % 2
                for k in range(NK):
                    bk = 6 + k // 8
                    P.add("pe", lambda e, b=b, k=k, bk=bk: e.transpose(
                        out=bank(bk).bitcast(BF16)[:, (k % 8) * 128:(k % 8 + 1) * 128],
                        in_=ysb[b][:, k * 128:(k + 1) * 128], identity=identb),
                        reads=[("ysb", b), "identb"], writes=[("ps", bk)])
                P.add("dve", lambda e, tt=tt, yb=yb: e.tensor_tensor(
                    out=yT[yb][:, :, tt * 128:(tt + 1) * 128],
                    in0=ps[:, 6 * 512:8 * 512].bitcast(BF16).rearrange("p (k t) -> p k t", t=128),
                    in1=gall[:, G_OUTN, :].unsqueeze(2).to_broadcast([128, NK, 128]), op=ALU.mult),
                    reads=[("ps", 6), ("ps", 7), "gall"], writes=[("yT", yb, tt)])

            for tt in range(4):
                front(0, tt)
                back(0, tt)
            pending = []
            for blk in range(NB_):
                def hook(dc, blk=blk):
                    if pending:
                        pending.pop(0)()
                    if blk + 1 < NB_:
                        if dc % 4 == 0:
                            front(blk + 1, dc // 4)
                        elif dc % 4 == 2:
                            back(blk + 1, dc // 4)

                yb = blk % 2
                steps = proj_postnorm(TB, NK,
                                      lambda dc, c: woutb[:, c, dc * 128:(dc + 1) * 128],
                                      lambda c, bs, bn, yb=yb: yT[yb][:, c, bs:bs + bn],
                                      lambda dc, c, bi, yb=yb: [("woutb", c)] + [("yT", yb, tt) for tt in range(4)],
                                      None, h1T, "h1T", blk * TB, h2T, "h2T", G_MIXPOST, 1.0,
                                      (sq, accv, rbc, fT, hTt), defer=True, hook=hook, fT2=fT2)
                while pending:
                    pending.pop(0)()
                pending = steps
            P.barrier()
            A.top = mark
            return pending

        STAGES = ["fm", "pre", "ffn1", "mix", "wout", "ffn2", "all"]
        sub = None
        if stop_after in ("m2a", "m2b", "m2c", "m2d"):
            DBG["cut"] = stop_after[2]
            stop_after = "m2"
        if stop_after in ("m0", "m1", "m2", "m3", "m3a", "mixonly"):
            DBG["stop"] = stop_after
            DBG["skip_ffn1"] = True
            if stop_after == "m3a":
                DBG["stop"] = "m3"
                DBG["nheads"] = 1
            stop_after = "mix"
        if stop_after in ("dn", "dn2", "p1"):
            DBG["stop"] = stop_after
            stop_after = "ffn1"
        if stop_after in ("gu",):
            sub = stop_after
            stop_after = "ffn1"
        lvl = STAGES.index(stop_after) if stop_after is not None else len(STAGES) - 1
        to_featmajor(x_in, h0T, "h0T", NT)
        to_featmajor(mem_in, memT0, "memT0", NMEM)
        if lvl == 1:
            mark = A.top
            hT_ = A.alloc([NK, 768], F32)
            xn_ = A.alloc([NK, 768], BF16)
            xf_ = A.alloc([NK, 768], F32)
            tmp_ = ([A.alloc([768], F32) for _ in range(2)], A.alloc([768], F32), A.alloc([768], F32))
            prenorm_T(h0T, "h0T", 0, 768, G_F1PRE, hT_, xn_, lambda k: ("xn_", k), tmp_)
            P.add("dve", lambda e: e.tensor_copy(out=xf_.rearrange("p a b -> p (a b)"), in_=xn_.rearrange("p a b -> p (a b)")),
                  reads=[("xn_", k) for k in range(NK)], writes=["xf_"])
            P.add("sp", lambda e: e.dma_start(out=h1T[:, :, 0:768].rearrange("k p t -> p k t"), in_=xf_), reads=["xf_"],
                  writes=["h1Tdbg"], dkey="dbg")
        pend = []
        if lvl >= 2 and not DBG.get("skip_ffn1"):
            pend = ffn_phase(h0T, "h0T", h1T, "h1T", [(0, 768), (768, 768), (1536, 768)], wgu1, wd1, G_F1PRE, G_F1POST, dbg_stop=sub)
        if lvl >= 3:
            mixer_phase(pend)
            pend = []
        if lvl >= 4:
            pend = wout_phase()
        if lvl >= 5:
            pend = ffn_phase(h2T, "h2T", h3T, "h3T", [(0, 768), (768, 640), (1408, 640)], wgu2, wd2, G_F2PRE, G_F2POST,
                             pre_pending=pend)
        if lvl >= 6:
            final_phase(h3T, "h3T", pend)
            pend = []
        for st_ in (pend or []):
            st_()
        P.barrier()
        P.emit(nc)
    return nc


def _prep_shared(inp):
    f = np.float32
    sh = {}

    def gu(w):
        w = np.asarray(w, f).reshape(NK, 128, 2, NCH, 128)
        return np.ascontiguousarray(w.transpose(3, 1, 0, 2, 4)).reshape(NCH, 128, NK, 256)

    def dn(w):
        w = np.asarray(w, f).reshape(NCH, 128, NK, 128)
        return np.ascontiguousarray(w.transpose(2, 1, 0, 3))

    sh["wgu1"] = gu(inp["ffn1_w_gate_up"][0])
    sh["wd1"] = dn(inp["ffn1_w_down"][0])
    sh["wgu2"] = gu(inp["ffn2_w_gate_up"][0])
    sh["wd2"] = dn(inp["ffn2_w_down"][0])
    w = np.asarray(inp["w_in"][0], f).reshape(NK, 128, 36, 128)
    sh["win"] = np.ascontiguousarray(w.transpose(2, 1, 0, 3))
    sh["wkv"] = np.ascontiguousarray(np.asarray(inp["w_mem_kv"][0], f))
    sh["wout"] = np.ascontiguousarray(np.asarray(inp["w_out"][0], f))
    outn = np.concatenate([inp["out_norm_na"][0], inp["out_norm_sg"][0], inp["out_norm_mem"][0]])
    gl = [inp["ffn1_norm_pre"][0], inp["ffn1_norm_post"][0], inp["mix_norm_pre"][0], inp["mem_norm"][0], outn,
          inp["mix_norm_post"][0], inp["ffn2_norm_pre"][0], inp["ffn2_norm_post"][0], inp["final_norm"][0]]
    gall = np.stack([np.asarray(g, f).reshape(NK, 128).T for g in gl], axis=1)
    sh["gall"] = np.ascontiguousarray(gall).reshape(128, 9 * 16)
    ws = np.asarray(inp["sg_w_spatial"][0], f)
    sh["wst"] = np.ascontiguousarray(ws.transpose(2, 0, 1)).reshape(128, 512)
    sh["bs"] = np.ascontiguousarray(np.asarray(inp["sg_b_spatial"][0], f).T)
    sh["lng"] = np.ascontiguousarray(np.broadcast_to(np.asarray(inp["sg_ln_gain"][0], f).reshape(1, 512), (128, 512)))
    sh["lnb"] = np.ascontiguousarray(np.broadcast_to(np.asarray(inp["sg_ln_bias"][0], f).reshape(1, 512), (128, 512)))
    sh["ident"] = np.eye(128, dtype=f)
    rpb = np.asarray(inp["na_rpb"][0], f)
    sh["bias_lo"] = build_bias_table(rpb, False)
    sh["bias_hi"] = build_bias_table(rpb, True)
    return sh


def make_in_maps(inp):
    sh = _prep_shared(inp)
    x = np.asarray(inp["x"], np.float32)
    mem = np.asarray(inp["mem"], np.float32)
    maps = []
    for core in range(8):
        b, half = core // 2, core % 2
        if half == 0:
            xl = x[b, 0:NT]
        else:
            xl = np.concatenate([x[b, 2048:4096], x[b, 1792:2048]], axis=0)
        m = {k: v for k, v in sh.items() if not k.startswith("bias_")}
        m["bias"] = sh["bias_hi"] if half else sh["bias_lo"]
        m["x"] = np.ascontiguousarray(xl)
        m["mem"] = np.ascontiguousarray(mem[b])
        maps.append(m)
    return maps


_NC_CACHE = {}


def kernel(**inputs):
    maps = make_in_maps(inputs)
    if "nc" not in _NC_CACHE:
        _NC_CACHE["nc"] = build_program()
    nc = _NC_CACHE["nc"]
    res = run_bass_kernel_spmd(nc, maps, core_ids=list(range(8)))
    outp = np.empty((4, 4096, D), dtype=np.float32)
    for core in range(8):
        b, half = core // 2, core % 2
        outp[b, half * 2048:(half + 1) * 2048] = res.results[core]["out"]
    return outp
```

```python
import contextlib
import numpy as np
import concourse.bass as bass
import concourse.mybir as mybir
from concourse.bass_utils import run_bass_kernel_spmd

F32 = mybir.dt.float32
BF16 = mybir.dt.bfloat16
AF = mybir.ActivationFunctionType
ALU = mybir.AluOpType

D = 2048
DFF = 5632
NCH = DFF // 128
NK = D // 128
NT = 2304
NOWN = 2048
NMEM = 256
EPS = 1e-6
NEG = -30000.0
G_F1PRE, G_F1POST, G_MIXPRE, G_MEM, G_OUTN, G_MIXPOST, G_F2PRE, G_F2POST, G_FINAL = range(9)


class Op:
    __slots__ = ("idx", "eng", "fn", "deps", "dkey", "dval", "sig", "sigval", "bar")


class Prog:
    ENGS = ("pe", "act", "dve", "pool", "sp")

    def __init__(self, same_engine_sync=True):
        self.ops = []
        self.last_w = {}
        self.readers = {}
        self.dma_count = {}
        self.same_engine_sync = same_engine_sync

    def add(self, eng, fn, reads=(), writes=(), dkey=None):
        op = Op()
        op.idx = len(self.ops)
        op.eng = eng
        op.fn = fn
        op.dkey = dkey
        op.bar = False
        op.sig = False
        op.sigval = 0
        deps = set()
        for r in reads:
            w = self.last_w.get(r)
            if w is not None:
                deps.add(w)
        for r in writes:
            w = self.last_w.get(r)
            if w is not None:
                deps.add(w)
            rd = self.readers.get(r)
            if rd:
                deps.update(rd.values())
        key = eng if dkey is None else ("dma", op.idx)
        for r in reads:
            self.readers.setdefault(r, {})[key] = op.idx
        for r in writes:
            self.last_w[r] = op.idx
            self.readers[r] = {}
        deps.discard(op.idx)
        op.deps = deps
        if dkey is not None:
            c = self.dma_count.get(dkey, 0) + 1
            self.dma_count[dkey] = c
            op.dval = 16 * c
        else:
            op.dval = 0
        self.ops.append(op)
        return op

    def barrier(self):
        op = Op()
        op.idx = len(self.ops)
        op.eng = None
        op.fn = None
        op.dkey = None
        op.bar = True
        op.deps = set()
        op.sig = False
        op.sigval = 0
        op.dval = 0
        self.ops.append(op)
        self.last_w = {}
        self.readers = {}

    def emit(self, nc):
        ops = self.ops
        last_on_eng = {}
        bar_snap = {}
        dma_state = {}
        for op in ops:
            if op.bar:
                for e, o in last_on_eng.items():
                    o.sig = True
                bar_snap[op.idx] = (dict(last_on_eng), dict(dma_state))
                continue
            for d in op.deps:
                dep = ops[d]
                if dep.dkey is not None:
                    continue
                if dep.eng == op.eng and (op.eng == "pe" or not self.same_engine_sync):
                    continue
                dep.sig = True
            if op.dkey is not None:
                dma_state[op.dkey] = op.dval
            else:
                last_on_eng[op.eng] = op
        cnt = {e: 0 for e in self.ENGS}
        for op in ops:
            if op.bar or op.dkey is not None:
                continue
            if op.sig:
                cnt[op.eng] += 1
                op.sigval = cnt[op.eng]
        per_eng = {e: [] for e in self.ENGS}
        for op in ops:
            if op.bar:
                for e in self.ENGS:
                    per_eng[e].append(op)
            else:
                per_eng[op.eng].append(op)
        dkeys = sorted(self.dma_count.keys(), key=str)
        with contextlib.ExitStack() as st:
            esem = {e: st.enter_context(nc.semaphore("se_" + e)) for e in self.ENGS}
            dsem = {k: st.enter_context(nc.semaphore("sd_%d" % i)) for i, k in enumerate(dkeys)}
            block = st.enter_context(nc.Block())

            def run_engine(ename, e):
                waited = {}

                def wait(sem, val, key):
                    if val <= 0 or waited.get(key, 0) >= val:
                        return
                    waited[key] = val
                    e.wait_ge(sem, val)

                for op in per_eng[ename]:
                    if op.bar:
                        le, ds = bar_snap[op.idx]
                        for x, o in le.items():
                            if x != ename or ename != "pe":
                                wait(esem[x], o.sigval, ("e", x))
                        for k, v in ds.items():
                            wait(dsem[k], v, ("d", k))
                        continue
                    for d in sorted(op.deps):
                        dep = ops[d]
                        if dep.dkey is not None:
                            wait(dsem[dep.dkey], dep.dval, ("d", dep.dkey))
                        else:
                            if dep.eng == ename and (ename == "pe" or not self.same_engine_sync):
                                continue
                            wait(esem[dep.eng], dep.sigval, ("e", dep.eng))
                    ins = op.fn(e)
                    if op.dkey is not None:
                        ins.then_inc(dsem[op.dkey], 16)
                    elif op.sig:
                        ins.then_inc(esem[ename], 1)

            @block.tensor
            def _(e):
                run_engine("pe", e)

            @block.scalar
            def _(e):
                run_engine("act", e)

            @block.vector
            def _(e):
                run_engine("dve", e)

            @block.gpsimd
            def _(e):
                run_engine("pool", e)

            @block.sync
            def _(e):
                run_engine("sp", e)


class Arena:
    def __init__(self, sb, cap):
        self.sb = sb
        self.cap = cap
        self.top = 0

    def alloc(self, shape, dtype):
        n = 1
        for s in shape:
            n *= s
        nbytes = n * (4 if dtype == F32 else 2)
        off = (self.top + 63) // 64 * 64
        self.top = off + nbytes
        assert self.top <= self.cap, ("SBUF arena overflow", self.top, self.cap)
        v = self.sb[:, off // 2:(off + nbytes) // 2]
        if dtype == F32:
            v = v.bitcast(F32)
        if len(shape) == 2:
            v = v.rearrange("p (a b) -> p a b", b=shape[1])
        elif len(shape) == 3:
            v = v.rearrange("p (a b c) -> p a b c", b=shape[1], c=shape[2])
        return v


def tok_blocks(T):
    out = []
    s = 0
    while s < T:
        n = min(512, T - s)
        out.append((s, n))
        s += n
    return out


def na_blocks(p):
    if p <= 1:
        return [16, 17, 0, 1, 2, 3]
    if p == 15:
        return [12, 13, 14, 15, 16, 17]
    return [p - 2, p - 1, p, p + 1, p + 2]


TBL_IDX = [0, 1] + [2] * 12 + [3, 4]
TBL_P = [0, 1, 2, 14, 15]


def build_bias_table(rpb, hi):
    tbl = np.full((5, 8, 128, 768), NEG, dtype=np.float32)
    col = np.arange(64)
    col_start = np.clip(col - 8, 0, 48)
    for ti, p in enumerate(TBL_P):
        blks = na_blocks(p)
        for s, c in enumerate(blks):
            for kp in range(2):
                lk = 2 * c + kp
                if hi:
                    gk = lk + 32 if lk < 32 else lk - 4
                else:
                    gk = lk
                for qp in range(2):
                    lq = 2 * p + qp
                    gq = lq + 32 if hi else lq
                    rs = min(max(gq - 4, 0), 56)
                    if not (rs <= gk < rs + 8):
                        continue
                    dr = gk - gq
                    kc = col[:, None]
                    qc = col[None, :]
                    valid = (kc >= col_start[None, :]) & (kc < col_start[None, :] + 16)
                    dc = np.clip(kc - qc, -15, 15)
                    vals = rpb[:, dr + 7, :][:, dc + 15]
                    sub = np.where(valid[None], vals, np.float32(NEG))
                    tbl[ti, :, kp * 64:(kp + 1) * 64, s * 128 + qp * 64: s * 128 + (qp + 1) * 64] = sub
    return tbl


class Ctx:
    pass


DBG = {}


def build_program(stop_after=None):
    nc = bass.Bass("TRN2", target_bir_lowering=False)

    def dram(name, shape, dtype=F32, kind="ExternalInput"):
        return nc.dram_tensor(name, list(shape), dtype, kind=kind).ap()

    x_in = dram("x", [NT, D])
    mem_in = dram("mem", [NMEM, D])
    wgu1 = dram("wgu1", [NCH, 128, NK, 256])
    wd1 = dram("wd1", [NK, 128, NCH, 128])
    wgu2 = dram("wgu2", [NCH, 128, NK, 256])
    wd2 = dram("wd2", [NK, 128, NCH, 128])
    win = dram("win", [36, 128, NK, 128])
    wkv = dram("wkv", [D, 1024])
    wout = dram("wout", [D, D])
    gall_in = dram("gall", [128, 9 * 16])
    bias_in = dram("bias", [5, 8, 128, 768])
    wst_in = dram("wst", [128, 4 * 128])
    bs_in = dram("bs", [128, 4])
    lng_in = dram("lng", [128, 512])
    lnb_in = dram("lnb", [128, 512])
    ident_in = dram("ident", [128, 128])
    out = dram("out", [NOWN, D], kind="ExternalOutput")
    dbg = stop_after is not None
    ikind = "ExternalOutput" if dbg else "Internal"
    h0T = dram("h0T", [NK, 128, NT], kind=ikind)
    h1T = dram("h1T", [NK, 128, NT], kind=ikind)
    h2T = dram("h2T", [NK, 128, NOWN], kind=ikind)
    h3T = dram("h3T", [NK, 128, NOWN], kind=ikind)
    memT0 = dram("memT0", [NK, 128, NMEM], kind="Internal")
    fscr = dram("fscr", [NK, 128, 768], kind="Internal")
    ys = dram("ys", [NOWN, D], kind=ikind)

    with contextlib.ExitStack() as st:
        SB_BYTES = 206 * 1024
        sb = st.enter_context(nc.sbuf_tensor("sb", [128, SB_BYTES // 2], BF16))
        ps = st.enter_context(nc.psum_tensor("ps", [128, 4096], F32))
        P = Prog()
        A = Arena(sb, SB_BYTES)
        C = Ctx()
        C.nc, C.P, C.A, C.ps = nc, P, A, ps

        def bank(b, n=512, off=0):
            return ps[:, b * 512 + off: b * 512 + off + n]

        identf = A.alloc([128], F32)
        identb = A.alloc([128], BF16)
        onesf = A.alloc([128], F32)
        gall = A.alloc([9, 16], F32)
        stats = A.alloc([64, 8], F32)
        P.add("sp", lambda e: e.dma_start(out=identf, in_=ident_in), writes=["identf"], dkey="c0")
        P.add("sp", lambda e: e.dma_start(out=gall.rearrange("p a b -> p (a b)"), in_=gall_in), writes=["gall"], dkey="c1")
        P.add("dve", lambda e: e.tensor_copy(out=identb, in_=identf), reads=["identf"], writes=["identb"])
        P.add("dve", lambda e: e.memset(onesf, 1.0), writes=["onesf"])
        stat_ctr = [0]

        def stat():
            i = stat_ctr[0] % 64
            stat_ctr[0] += 1
            return stats[:, i, :], ("st", i)

        fTp = [A.alloc([768], F32) for _ in range(2)]
        hTtp = [A.alloc([768], F32) for _ in range(2)]
        rbcBp = A.alloc([768], F32)
        persist_top = A.top

        def to_featmajor(src, dst, dstname, ntok):
            mark = A.top
            NXB = 3
            xt = [A.alloc([D], F32) for _ in range(NXB)]
            stg = [A.alloc([NK, 128], F32) for _ in range(2)]
            ntile = ntok // 128

            def ld(j):
                xb = j % NXB
                P.add("sp", lambda e, j=j, xb=xb: e.dma_start(out=xt[xb], in_=src[j * 128:(j + 1) * 128, :]),
                      writes=[("xt", xb)], dkey=("xt", xb))

            ld(0)
            if ntile > 1:
                ld(1)
            for j in range(ntile):
                b = j % 2
                xb = j % NXB
                if j + 2 < ntile:
                    ld(j + 2)
                for k in range(NK):
                    bk = 4 * b + k // 4
                    P.add("pe", lambda e, xb=xb, k=k, bk=bk: e.transpose(
                        out=bank(bk, 128, (k % 4) * 128), in_=xt[xb][:, k * 128:(k + 1) * 128], identity=identf),
                        reads=[("xt", xb), "identf"], writes=[("ps", bk)])
                eng = "act" if b == 0 else "dve"
                if eng == "act":
                    P.add("act", lambda e, b=b: e.activation(
                        out=stg[b].rearrange("p a b -> p (a b)"), in_=ps[:, 4 * b * 512:(4 * b + 4) * 512], func=AF.Copy),
                        reads=[("ps", 4 * b + i) for i in range(4)], writes=[("stg", b)])
                else:
                    P.add("dve", lambda e, b=b: e.tensor_copy(
                        out=stg[b].rearrange("p a b -> p (a b)"), in_=ps[:, 4 * b * 512:(4 * b + 4) * 512]),
                        reads=[("ps", 4 * b + i) for i in range(4)], writes=[("stg", b)])
                P.add("sp", lambda e, j=j, b=b: e.dma_start(
                    out=dst[:, :, j * 128:(j + 1) * 128].rearrange("k p t -> p k t"), in_=stg[b]),
                    reads=[("stg", b)], writes=[(dstname, k, j) for k in range(NK)], dkey=("stg", b))
            P.barrier()
            A.top = mark

        def colnorm_stats(accv, T, factor, rbc, totbanks, tag=""):
            s1 = 1.0 / (D * factor * factor)
            s2 = EPS / (factor * factor)
            for bi, (bs, bn) in enumerate(tok_blocks(T)):
                bk = totbanks[bi]
                P.add("pe", lambda e, bk=bk, bs=bs, bn=bn: e.matmul(
                    bank(bk, bn), lhsT=onesf, rhs=accv[:, bs:bs + bn], start=True, stop=True),
                    reads=[tag + "acc", "onesf"], writes=[("ps", bk)])
                P.add("dve", lambda e, bk=bk, bs=bs, bn=bn: e.tensor_scalar(
                    out=rbc[:, bs:bs + bn], in0=bank(bk, bn), scalar1=s1, scalar2=s2, op0=ALU.mult, op1=ALU.add),
                    reads=[("ps", bk)], writes=[(tag + "rbc", bi)])
            rr = [(tag + "rbc", bi) for bi in range(len(tok_blocks(T)))]
            P.add("act", lambda e: e.activation(out=rbc[:, 0:T], in_=rbc[:, 0:T], func=AF.Sqrt), reads=rr, writes=rr)
            P.add("dve", lambda e: e.reciprocal(out=rbc[:, 0:T], in_=rbc[:, 0:T]), reads=rr, writes=rr)
            return rr

        def load_and_stats(src, srcname, tok0, T, hT, sq, accv, rbc, factor, totbanks, tag="", rname="R",
                           do_load=True, do_stats=True):
            tiles = range(tok0 // 128, (tok0 + T) // 128)
            if do_load:
                for q in range(4):
                    P.add("sp", lambda e, q=q: e.dma_start(
                        out=hT[:, 4 * q:4 * q + 4, 0:T],
                        in_=src[4 * q:4 * q + 4, :, tok0:tok0 + T].rearrange("k p t -> p k t")),
                        reads=[(srcname, k, j) for k in range(4 * q, 4 * q + 4) for j in tiles],
                        writes=[(rname, k) for k in range(4 * q, 4 * q + 4)], dkey=(tag + "hT", rname, q))
            if not do_stats:
                return None
            for k in range(NK):
                if k == 0:
                    P.add("act", lambda e: e.activation(out=accv[:, 0:T], in_=hT[:, 0, 0:T], func=AF.Square),
                          reads=[(rname, 0)], writes=[tag + "acc"])
                else:
                    sbi = k % 2
                    P.add("act", lambda e, k=k, sbi=sbi: e.activation(out=sq[sbi][:, 0:T], in_=hT[:, k, 0:T], func=AF.Square),
                          reads=[(rname, k)], writes=[(tag + "sq", sbi)])
                    P.add("dve", lambda e, sbi=sbi: e.tensor_tensor(
                        out=accv[:, 0:T], in0=accv[:, 0:T], in1=sq[sbi][:, 0:T], op=ALU.add),
                        reads=[tag + "acc", (tag + "sq", sbi)], writes=[tag + "acc"])
            return colnorm_stats(accv, T, factor, rbc, totbanks, tag)

        def prenorm_T(src, srcname, tok0, T, gi, Rbuf, dst, dstres, tmp, tag=""):
            hT = Rbuf
            sq, accv, rbc = tmp
            rr = load_and_stats(src, srcname, tok0, T, hT, sq, accv, rbc, 1.0, (6, 7), tag)
            for k in range(NK):
                P.add("dve", lambda e, k=k: e.scalar_tensor_tensor(
                    out=dst[:, k, 0:T], in0=hT[:, k, 0:T], scalar=gall[:, gi, k:k + 1], in1=rbc[:, 0:T],
                    op0=ALU.mult, op1=ALU.mult),
                    reads=[("R", k), "gall"] + rr, writes=[dstres(k)])

        def proj_postnorm(T, nc_in, lhs_fn, rhs_fn, rd_fn, wload_fn, src, srcname, tok0, dst, dstname,
                          gi_post, factor, tmp, defer=False, hook=None, fT2=None):
            sq, accv, rbc, fT, hTt = tmp
            tag = "B"
            f2 = fT2 if fT2 is not None else fT
            f2n = "fT2" if fT2 is not None else "fT"
            blocks = tok_blocks(T)
            tiles = range(tok0 // 128, (tok0 + T) // 128)
            for dc in range(NK):
                s = dc % 2
                if wload_fn is not None:
                    wload_fn(dc)
                for bi, (bs, bn) in enumerate(blocks):
                    bk = 2 * s + bi
                    for c in range(nc_in):
                        P.add("pe", lambda e, dc=dc, c=c, bk=bk, bs=bs, bn=bn: e.matmul(
                            bank(bk, bn), lhsT=lhs_fn(dc, c), rhs=rhs_fn(c, bs, bn),
                            start=(c == 0), stop=(c == nc_in - 1)),
                            reads=rd_fn(dc, c, bi), writes=[("ps", bk)])
                if hook is not None:
                    hook(dc)
                for bi, (bs, bn) in enumerate(blocks):
                    bk = 2 * s + bi
                    P.add("act", lambda e, s=s, bk=bk, bs=bs, bn=bn: e.activation(
                        out=fT[s][:, bs:bs + bn], in_=bank(bk, bn), func=AF.Copy),
                        reads=[("ps", bk)], writes=[("fT", s, bi)])
                fr = [("fT", s, bi) for bi in range(len(blocks))]
                P.add("sp", lambda e, dc=dc, s=s: e.dma_start(out=fscr[dc, :, 0:T], in_=fT[s][:, 0:T]),
                      reads=fr, writes=[("fscr", dc)], dkey=("fT", s))
                if dc == 0:
                    P.add("act", lambda e, s=s: e.activation(out=accv[:, 0:T], in_=fT[s][:, 0:T], func=AF.Square),
                          reads=fr, writes=[tag + "acc"])
                else:
                    P.add("act", lambda e, s=s: e.activation(out=sq[s][:, 0:T], in_=fT[s][:, 0:T], func=AF.Square),
                          reads=fr, writes=[(tag + "sq", s)])
                    P.add("dve", lambda e, s=s: e.tensor_tensor(
                        out=accv[:, 0:T], in0=accv[:, 0:T], in1=sq[s][:, 0:T], op=ALU.add),
                        reads=[tag + "acc", (tag + "sq", s)], writes=[tag + "acc"])
            rr = colnorm_stats(accv, T, factor, rbc, (4, 5), tag)

            def loads(dc):
                s = dc % 2
                fw_ = [(f2n, s, bi) for bi in range(len(blocks))]
                P.add("sp", lambda e, dc=dc, s=s: e.dma_start(out=f2[s][:, 0:T], in_=fscr[dc, :, 0:T]),
                      reads=[("fscr", dc)], writes=fw_, dkey=(f2n, s))
                P.add("sp", lambda e, dc=dc, s=s: e.dma_start(out=hTt[s][:, 0:T], in_=src[dc, :, tok0:tok0 + T]),
                      reads=[(srcname, dc, j) for j in tiles], writes=[("hTt", s)], dkey=("hTt", s))

            def step(dc):
                s = dc % 2
                fw_ = [(f2n, s, bi) for bi in range(len(blocks))]
                if dc == 0:
                    loads(0)
                P.add("dve", lambda e, dc=dc, s=s: e.scalar_tensor_tensor(
                    out=f2[s][:, 0:T], in0=f2[s][:, 0:T], scalar=gall[:, gi_post, dc:dc + 1], in1=rbc[:, 0:T],
                    op0=ALU.mult, op1=ALU.mult),
                    reads=fw_ + rr + ["gall"], writes=fw_)
                if dc + 1 < NK:
                    loads(dc + 1)
                P.add("dve", lambda e, s=s: e.tensor_tensor(
                    out=hTt[s][:, 0:T], in0=f2[s][:, 0:T], in1=hTt[s][:, 0:T], op=ALU.add),
                    reads=fw_ + [("hTt", s)], writes=[("hTt", s)])
                P.add("sp", lambda e, dc=dc, s=s: e.dma_start(out=dst[dc, :, tok0:tok0 + T], in_=hTt[s][:, 0:T]),
                      reads=[("hTt", s)], writes=[(dstname, dc, j) for j in tiles], dkey=("hTt", s))

            steps = [(lambda dc=dc: step(dc)) for dc in range(NK)]
            if defer:
                return steps
            for st_ in steps:
                st_()
            return []

        def ffn_phase(src, srcname, dst, dstname, passes, wgu, wd, gi_pre, gi_post, dbg_stop=None, pre_pending=None):
            mark = A.top
            TM = 768
            Rb = A.alloc([NCH * TM], BF16)
            aT = Rb.rearrange("p (c t) -> p c t", t=TM)
            XW = A.alloc([NK * TM + 3 * NK * 256], BF16)
            hT = XW.bitcast(F32).rearrange("p (k t) -> p k t", t=TM)
            xnT = XW[:, 0:NK * TM].rearrange("p (k t) -> p k t", t=TM)
            wgub = [XW[:, NK * TM + i * NK * 256: NK * TM + (i + 1) * NK * 256].rearrange("p (k c) -> p k c", c=256)
                    for i in range(3)]
            wdb = [A.alloc([NCH, 128], BF16) for _ in range(2)]
            stmp = [A.alloc([TM], F32) for _ in range(2)]
            sqA = [A.alloc([TM], F32) for _ in range(2)]
            accA = A.alloc([TM], F32)
            rbcA = A.alloc([TM], F32)
            sqB = [A.alloc([TM], F32) for _ in range(2)]
            accB = A.alloc([TM], F32)
            rbcB = rbcBp
            fT = [A.alloc([TM], F32) for _ in range(2)]
            hTt = hTtp
            wctr = [0]

            def ares(i):
                return ("aT", i)

            def halias(k):
                if k < 8:
                    return [("xnT", 2 * k), ("xnT", 2 * k + 1)]
                off = 3072 * k - 24576
                return [("wgu", s_) for s_ in range(off // 8192, (off + 3071) // 8192 + 1)]

            def pn_loads(tok0, T):
                tiles = range(tok0 // 128, (tok0 + T) // 128)
                for q in range(4):
                    al = []
                    for k in range(4 * q, 4 * q + 4):
                        al += halias(k)
                    P.add("sp", lambda e, q=q, tok0=tok0, T=T: e.dma_start(
                        out=hT[:, 4 * q:4 * q + 4, 0:T],
                        in_=src[4 * q:4 * q + 4, :, tok0:tok0 + T].rearrange("k p t -> p k t")),
                        reads=[(srcname, k, j) for k in range(4 * q, 4 * q + 4) for j in tiles],
                        writes=[("R", k) for k in range(4 * q, 4 * q + 4)] + sorted(set(al)), dkey=("AhT", "R", q))

            def pn_step(k, T):
                al = halias(k)
                if k == 0:
                    P.add("act", lambda e: e.activation(out=accA[:, 0:T], in_=hT[:, 0, 0:T], func=AF.Square),
                          reads=[("R", 0)] + al, writes=["Aacc"])
                else:
                    sbi = k % 2
                    P.add("act", lambda e, k=k, sbi=sbi: e.activation(out=sqA[sbi][:, 0:T], in_=hT[:, k, 0:T], func=AF.Square),
                          reads=[("R", k)] + al, writes=[("Asq", sbi)])
                    P.add("dve", lambda e, sbi=sbi: e.tensor_tensor(
                        out=accA[:, 0:T], in0=accA[:, 0:T], in1=sqA[sbi][:, 0:T], op=ALU.add),
                        reads=["Aacc", ("Asq", sbi)], writes=["Aacc"])

            def pn_finish(T):
                rr = colnorm_stats(accA, T, 1.0, rbcA, (6, 7), "A")
                for k in range(NK):
                    P.add("dve", lambda e, k=k: e.scalar_tensor_tensor(
                        out=xnT[:, k, 0:T], in0=hT[:, k, 0:T], scalar=gall[:, gi_pre, k:k + 1], in1=rbcA[:, 0:T],
                        op0=ALU.mult, op1=ALU.mult),
                        reads=[("R", k), "gall"] + rr + halias(k), writes=[("xnT", k)])

            def do_prenorm(tok0, T):
                pn_loads(tok0, T)
                for k in range(NK):
                    pn_step(k, T)
                pn_finish(T)

            pending = list(pre_pending or [])
            do_prenorm(*passes[0])
            for pi, (tok0, T) in enumerate(passes):
                blocks = tok_blocks(T)
                for i in range(NCH):
                    s = wctr[0] % 3
                    wctr[0] += 1
                    P.add("pool", lambda e, i=i, s=s: e.dma_start(out=wgub[s], in_=wgu[i]),
                          writes=[("wgu", s)], dkey=("wgu", s))
                    pset = i % 2
                    for half in range(2):
                        for bi, (bs, bn) in enumerate(blocks):
                            bk = 4 * pset + 2 * half + bi
                            for k in range(NK):
                                P.add("pe", lambda e, s=s, half=half, k=k, bk=bk, bs=bs, bn=bn: e.matmul(
                                    bank(bk, bn), lhsT=wgub[s][:, k, half * 128:(half + 1) * 128],
                                    rhs=xnT[:, k, bs:bs + bn], start=(k == 0), stop=(k == NK - 1)),
                                    reads=[("wgu", s), ("xnT", k)], writes=[("ps", bk)])
                    for bi, (bs, bn) in enumerate(blocks):
                        gb = 4 * pset + bi
                        ub = 4 * pset + 2 + bi
                        P.add("act", lambda e, pset=pset, gb=gb, bs=bs, bn=bn: e.activation(
                            out=stmp[pset][:, bs:bs + bn], in_=bank(gb, bn), func=AF.Silu),
                            reads=[("ps", gb)], writes=[("stmp", pset, bi)])
                        P.add("dve", lambda e, i=i, pset=pset, ub=ub, bs=bs, bn=bn: e.tensor_tensor(
                            out=aT[:, i, bs:bs + bn], in0=bank(ub, bn), in1=stmp[pset][:, bs:bs + bn], op=ALU.mult),
                            reads=[("ps", ub), ("stmp", pset, bi)], writes=[ares(i)])
                    if pending and i >= 3 and i % 2 == 1:
                        pending.pop(0)()
                while pending:
                    pending.pop(0)()

                def wload(dc):
                    s = dc % 2
                    P.add("pool", lambda e, dc=dc, s=s: e.dma_start(out=wdb[s], in_=wd[dc]),
                          writes=[("wd", s)], dkey=("wd", s))

                nxt = passes[pi + 1] if pi + 1 < len(passes) else None

                def hook(dc, nxt=nxt):
                    if nxt is None:
                        return
                    if dc == 0:
                        pn_loads(*nxt)
                    if dc >= 2:
                        pn_step(dc - 2, nxt[1])

                pending = proj_postnorm(T, NCH,
                                        lambda dc, c: wdb[dc % 2][:, c, :],
                                        lambda c, bs, bn: aT[:, c, bs:bs + bn],
                                        lambda dc, c, bi: [("wd", dc % 2), ares(c)],
                                        wload, src, srcname, tok0, dst, dstname, gi_post, 0.5,
                                        (sqB, accB, rbcB, fT, hTt), defer=True, fT2=fTp, hook=hook)
                if nxt is not None:
                    pn_step(14, nxt[1])
                    pn_step(15, nxt[1])
                    pn_finish(nxt[1])
            P.barrier()
            A.top = mark
            return pending

        def final_phase(src, srcname, pending=None):
            mark = A.top
            TM = 512
            hTf = [A.alloc([NK, TM], F32) for _ in range(2)]
            sqs = [[A.alloc([TM], F32) for _ in range(2)] for _ in range(2)]
            accs = [A.alloc([TM], F32) for _ in range(2)]
            rbcs = [A.alloc([TM], F32) for _ in range(2)]
            ot = [A.alloc([D], F32) for _ in range(2)]
            blks = [(0, 512), (512, 512), (1024, 512), (1536, 512)]

            def fload(i):
                tok0, T = blks[i]
                load_and_stats(src, srcname, tok0, T, hTf[i % 2], None, None, None, 1.0, (6, 7), "F%d" % (i % 2),
                               ("RF", i % 2), do_load=True, do_stats=False)

            def fstats(i):
                tok0, T = blks[i]
                q = i % 2
                hT = hTf[q]
                rn = ("RF", q)
                rr = load_and_stats(src, srcname, tok0, T, hT, sqs[q], accs[q], rbcs[q], 1.0, (6, 7), "F%d" % q, rn,
                                    do_load=False, do_stats=True)
                for k in range(NK):
                    P.add("dve", lambda e, k=k, hT=hT, q=q: e.scalar_tensor_tensor(
                        out=hT[:, k, 0:T], in0=hT[:, k, 0:T], scalar=gall[:, G_FINAL, k:k + 1], in1=rbcs[q][:, 0:T],
                        op0=ALU.mult, op1=ALU.mult),
                        reads=[(rn, k), "gall"] + rr, writes=[(rn, k)])

            fload(0)
            fstats(0)
            jc = 0
            pending = list(pending or [])
            for i, (tok0, T) in enumerate(blks):
                hT = hTf[i % 2]
                rn = ("RF", i % 2)
                if i == 1:
                    while pending:
                        pending.pop(0)()
                if i + 1 < len(blks):
                    fload(i + 1)
                    fstats(i + 1)
                for j in range(T // 128):
                    b = jc % 2
                    jc += 1
                    for k in range(NK):
                        bk = 4 * b + k // 4
                        P.add("pe", lambda e, j=j, k=k, bk=bk, hT=hT: e.transpose(
                            out=bank(bk, 128, (k % 4) * 128), in_=hT[:, k, j * 128:(j + 1) * 128], identity=identf),
                            reads=[(rn, k), "identf"], writes=[("ps", bk)])
                    if b == 0:
                        P.add("act", lambda e, b=b: e.activation(
                            out=ot[b], in_=ps[:, 4 * b * 512:(4 * b + 4) * 512], func=AF.Copy),
                            reads=[("ps", 4 * b + i2) for i2 in range(4)], writes=[("ot", b)])
                    else:
                        P.add("dve", lambda e, b=b: e.tensor_copy(out=ot[b], in_=ps[:, 4 * b * 512:(4 * b + 4) * 512]),
                              reads=[("ps", 4 * b + i2) for i2 in range(4)], writes=[("ot", b)])
                    P.add("sp", lambda e, j=j, b=b, tok0=tok0: e.dma_start(
                        out=out[tok0 + j * 128: tok0 + (j + 1) * 128, :], in_=ot[b]),
                        reads=[("ot", b)], writes=[("out", tok0 // 128 + j)], dkey=("ot", b))
            P.barrier()
            A.top = mark

        def mixer_phase(pending=None):
            mark0 = A.top
            TM = 768
            xn2T = A.alloc([NK, NT], BF16)
            mark = A.top
            hT = A.alloc([NK, TM], F32)
            sq = [A.alloc([TM], F32) for _ in range(2)]
            accv = A.alloc([TM], F32)
            rbc = A.alloc([TM], F32)
            memT = A.alloc([NK, NMEM], BF16)
            pending = list(pending or [])
            for bi3 in range(3):
                tok0 = bi3 * 768
                if bi3 == 2:
                    while pending:
                        pending.pop(0)()
                prenorm_T(h1T, "h1T", tok0, 768, G_MIXPRE, hT, xn2T[:, :, tok0:tok0 + 768],
                          lambda k, bi3=bi3: ("xn2T", k, bi3), (sq, accv, rbc))
                for _ in range(8):
                    if pending:
                        pending.pop(0)()
            prenorm_T(memT0, "memT0", 0, NMEM, G_MEM, hT, memT, lambda k: ("memT", k), (sq, accv, rbc))
            P.barrier()
            if DBG.get("stop") == "m0":
                return
            wkvb = A.alloc([NK, 1024], BF16)
            for k in range(NK):
                P.add("pool", lambda e, k=k: e.dma_start(out=wkvb[:, k, :], in_=wkv[k * 128:(k + 1) * 128, :]),
                      writes=[("wkvb", k)], dkey=("wkvb", k))
            A_keep = A.top
            A.top = mark
            A.top = A_keep
            kmT = A.alloc([4, NMEM], BF16)
            vm = A.alloc([2, 4, 130], BF16)
            P.add("dve", lambda e: e.memset(vm.rearrange("p a b c -> p (a b c)"), 1.0), writes=["vm"])
            for h in range(4):
                bk = h % 2
                for k in range(NK):
                    P.add("pe", lambda e, h=h, k=k, bk=bk: e.matmul(
                        bank(bk, NMEM), lhsT=wkvb[:, k, h * 128:(h + 1) * 128], rhs=memT[:, k, :],
                        start=(k == 0), stop=(k == NK - 1)),
                        reads=[("wkvb", k), ("memT", k)], writes=[("ps", bk)])
                P.add("act", lambda e, h=h, bk=bk: e.activation(out=kmT[:, h, :], in_=bank(bk, NMEM), func=AF.Copy),
                      reads=[("ps", bk)], writes=[("kmT", h)])
            for mt in range(2):
                bk = 2 + mt
                for k in range(NK):
                    P.add("pe", lambda e, mt=mt, k=k, bk=bk: e.matmul(
                        bank(bk, 512), lhsT=memT[:, k, mt * 128:(mt + 1) * 128], rhs=wkvb[:, k, 512:1024],
                        start=(k == 0), stop=(k == NK - 1)),
                        reads=[("wkvb", k), ("memT", k)], writes=[("ps", bk)])
                P.add("dve", lambda e, mt=mt, bk=bk: e.tensor_copy(
                    out=vm[:, mt, :, 0:128], in_=bank(bk, 512).rearrange("p (h d) -> p h d", d=128)),
                    reads=[("ps", bk), "vm"], writes=["vm"])
            P.barrier()
            if DBG.get("stop") == "m1":
                return
            Bsub = Arena(sb, A_keep)
            Bsub.top = mark
            def balloc(shape, dt):
                try:
                    return Bsub.alloc(shape, dt)
                except AssertionError:
                    return A.alloc(shape, dt)

            winb = [balloc([NK, 128], BF16) for _ in range(4)]
            qT = [balloc([NOWN], BF16) for _ in range(2)]
            kT = [balloc([NT], BF16) for _ in range(2)]
            vT = balloc([NT], BF16)
            V = [balloc([18, 130], BF16) for _ in range(2)]
            btile = [balloc([768], F32) for _ in range(2)]
            sbx = [balloc([768], F32) for _ in range(2)]
            PT = [balloc([768], BF16) for _ in range(2)]
            yh = [balloc([16, 128], F32) for _ in range(2)]
            PTm = [balloc([2, 512], BF16) for _ in range(2)]
            osb = [balloc([130], F32) for _ in range(2)]
            for b in range(2):
                P.add("dve", lambda e, b=b: e.memset(V[b].rearrange("p a b -> p (a b)"), 1.0), writes=[("V", b)])
            wctr = [0]
            ectr = [0]

            def projT(group, ntok, scale, dstv, dstres):
                s = wctr[0] % 4
                wctr[0] += 1
                P.add("pool", lambda e, s=s: e.dma_start(out=winb[s], in_=win[group]),
                      writes=[("winb", s)], dkey=("winb", s))
                for bi, (bs, bn) in enumerate(tok_blocks(ntok)):
                    bk = ectr[0] % 2
                    ectr[0] += 1
                    for k in range(NK):
                        P.add("pe", lambda e, s=s, k=k, bk=bk, bs=bs, bn=bn: e.matmul(
                            bank(bk, bn), lhsT=winb[s][:, k, :], rhs=xn2T[:, k, bs:bs + bn],
                            start=(k == 0), stop=(k == NK - 1)),
                            reads=[("winb", s)] + [("xn2T", k, b3) for b3 in range(3)], writes=[("ps", bk)])
                    if bk == 0:
                        P.add("act", lambda e, bk=bk, bs=bs, bn=bn: e.activation(
                            out=dstv[:, bs:bs + bn], in_=bank(bk, bn), func=AF.Copy, scale=float(scale)),
                            reads=[("ps", bk)], writes=[(dstres, bi)])
                    else:
                        P.add("dve", lambda e, bk=bk, bs=bs, bn=bn: e.tensor_scalar(
                            out=dstv[:, bs:bs + bn], in0=bank(bk, bn), scalar1=float(scale), scalar2=None, op0=ALU.mult),
                            reads=[("ps", bk)], writes=[(dstres, bi)])
                return [(dstres, bi) for bi in range(len(tok_blocks(ntok)))]

            SC = 128.0 ** -0.5
            hctr = [0]

            def finish_head(yhb, colbase):
                b = yhb
                P.add("sp", lambda e, b=b, colbase=colbase: e.dma_start(
                    out=ys[:, colbase:colbase + 128].rearrange("(t p) c -> p t c", p=128), in_=yh[b]),
                    reads=[("yh", b)], writes=[("ys", colbase)], dkey=("yh", b))

            for hm in range(4):
                hb = hctr[0] % 2
                hctr[0] += 1
                qres = projT(32 + hm, NOWN, SC, qT[hb], ("qT", hb))
                if DBG.get("cut") == "a":
                    P.barrier()
                    return
                for tb in range(4):
                    pb = tb % 2
                    for j in range(2):
                        bk = 4 + 2 * pb + j
                        P.add("pe", lambda e, hm=hm, hb=hb, tb=tb, j=j, bk=bk: e.matmul(
                            bank(bk, 512), lhsT=kmT[:, hm, j * 128:(j + 1) * 128], rhs=qT[hb][:, tb * 512:(tb + 1) * 512],
                            start=True, stop=True),
                            reads=[("kmT", hm), (("qT", hb), tb)], writes=[("ps", bk)])
                        P.add("act", lambda e, pb=pb, j=j, bk=bk: e.activation(
                            out=PTm[pb][:, j, :], in_=bank(bk, 512), func=AF.Exp),
                            reads=[("ps", bk)], writes=[("PTm", pb, j)])
                    if DBG.get("cut") == "b":
                        P.barrier()
                        return
                    for tt in range(4):
                        tile = tb * 4 + tt
                        ob = tile % 2
                        for j in range(2):
                            P.add("pe", lambda e, hm=hm, pb=pb, tt=tt, j=j, ob=ob: e.matmul(
                                bank(2 + ob, 130), lhsT=PTm[pb][:, j, tt * 128:(tt + 1) * 128], rhs=vm[:, j, hm, 0:130],
                                start=(j == 0), stop=(j == 1)),
                                reads=[("PTm", pb, j), "vm"], writes=[("ps", 2 + ob)])
                        rc, rcres = stat()
                        P.add("act", lambda e, ob=ob: e.activation(out=osb[ob], in_=bank(2 + ob, 130), func=AF.Copy),
                              reads=[("ps", 2 + ob)], writes=[("osb", ob)])
                        P.add("dve", lambda e, ob=ob, rc=rc: e.reciprocal(out=rc[:, 0:1], in_=osb[ob][:, 128:129]),
                              reads=[("osb", ob)], writes=[rcres])
                        P.add("dve", lambda e, ob=ob, rc=rc, hb=hb, tile=tile: e.tensor_scalar(
                            out=yh[hb][:, tile, :], in0=osb[ob][:, 0:128], scalar1=rc[:, 0:1], scalar2=None, op0=ALU.mult),
                            reads=[("osb", ob), rcres], writes=[("yh", hb)])
                if DBG.get("cut") == "c":
                    P.barrier()
                    return
                finish_head(hb, 1536 + hm * 128)
                if DBG.get("cut") == "d":
                    P.barrier()
                    return

            if DBG.get("stop") == "m2":
                P.barrier()
                return
            for h in range(DBG.get("nheads", 8)):
                hb = hctr[0] % 2
                hctr[0] += 1
                qres = projT(h, NOWN, SC, qT[hb], ("qT", hb))
                kres = projT(8 + h, NT, 1.0, kT[hb], ("kT", hb))
                vres = projT(16 + h, NT, 1.0, vT, "vT")
                for c0 in range(0, 18, 8):
                    bk = ectr[0] % 2
                    ectr[0] += 1
                    n8 = min(8, 18 - c0)
                    for c in range(c0, c0 + n8):
                        slot = c - c0
                        P.add("pe", lambda e, c=c, slot=slot, bk=bk: e.transpose(
                            out=bank(bk).bitcast(BF16)[:, slot * 128:(slot + 1) * 128], in_=vT[:, c * 128:(c + 1) * 128],
                            identity=identb),
                            reads=vres + ["identb"], writes=[("ps", bk)])
                    P.add("dve", lambda e, c0=c0, n8=n8, bk=bk, hb=hb: e.tensor_copy(
                        out=V[hb][:, c0:c0 + n8, 0:128],
                        in_=bank(bk).bitcast(BF16)[:, 0:n8 * 128].rearrange("p (c d) -> p c d", d=128)),
                        reads=[("ps", bk)], writes=[("V", hb)])
                def na_scores(p):
                    blks = na_blocks(p)
                    ub = p % 2
                    P.add("sp", lambda e, p=p, h=h, ub=ub: e.dma_start(out=btile[ub], in_=bias_in[TBL_IDX[p], h]),
                          writes=[("btile", ub)], dkey=("btile", ub))
                    for s, c in enumerate(blks):
                        bk = 4 + 2 * ub + s // 4
                        P.add("pe", lambda e, hb=hb, p=p, s=s, c=c, bk=bk: e.matmul(
                            bank(bk, 128, (s % 4) * 128), lhsT=kT[hb][:, c * 128:(c + 1) * 128],
                            rhs=qT[hb][:, p * 128:(p + 1) * 128], start=True, stop=True),
                            reads=kres + qres, writes=[("ps", bk)])

                na_scores(0)
                for p in range(16):
                    blks = na_blocks(p)
                    nb = len(blks)
                    ub = p % 2
                    if p + 1 < 16:
                        na_scores(p + 1)
                    P.add("dve", lambda e, ub=ub, nb=nb: e.tensor_tensor(
                        out=sbx[ub][:, 0:nb * 128], in0=ps[:, (4 + 2 * ub) * 512:(4 + 2 * ub) * 512 + nb * 128],
                        in1=btile[ub][:, 0:nb * 128], op=ALU.add),
                        reads=[("ps", 4 + 2 * ub), ("ps", 5 + 2 * ub), ("btile", ub)], writes=[("sbx", ub)])
                    P.add("act", lambda e, ub=ub, nb=nb: e.activation(
                        out=PT[ub][:, 0:nb * 128], in_=sbx[ub][:, 0:nb * 128], func=AF.Exp),
                        reads=[("sbx", ub)], writes=[("PT", ub)])
                    for s, c in enumerate(blks):
                        P.add("pe", lambda e, ub=ub, s=s, c=c, hb=hb, nb=nb: e.matmul(
                            bank(2 + ub, 130), lhsT=PT[ub][:, s * 128:(s + 1) * 128], rhs=V[hb][:, c, 0:130],
                            start=(s == 0), stop=(s == nb - 1)),
                            reads=[("PT", ub), ("V", hb)], writes=[("ps", 2 + ub)])
                    rc, rcres = stat()
                    P.add("act", lambda e, ub=ub: e.activation(out=osb[ub], in_=bank(2 + ub, 130), func=AF.Copy),
                          reads=[("ps", 2 + ub)], writes=[("osb", ub)])
                    P.add("dve", lambda e, ub=ub, rc=rc: e.reciprocal(out=rc[:, 0:1], in_=osb[ub][:, 128:129]),
                          reads=[("osb", ub)], writes=[rcres])
                    P.add("dve", lambda e, ub=ub, rc=rc, hb=hb, p=p: e.tensor_scalar(
                        out=yh[hb][:, p, :], in0=osb[ub][:, 0:128], scalar1=rc[:, 0:1], scalar2=None, op0=ALU.mult),
                        reads=[("osb", ub), rcres], writes=[("yh", hb)])
                finish_head(hb, h * 128)
            P.barrier()
            if DBG.get("stop") == "m3":
                return

            Bsub.top = mark
            A.top = A_keep
            wz = balloc([NK, 1024], BF16)
            wsTf = balloc([4, 128], F32)
            wsTb = balloc([4, 128], BF16)
            bsv = balloc([4], F32)
            lng = balloc([512], F32)
            lnb = balloc([512], F32)
            t1 = [balloc([1024], F32) for _ in range(2)]
            ge = [balloc([1024], F32) for _ in range(2)]
            vln = [balloc([512], BF16) for _ in range(2)]
            ysg = [balloc([512], F32) for _ in range(2)]
            for g8 in range(8):
                P.add("pool", lambda e, g8=g8: e.dma_start(out=wz[:, :, g8 * 128:(g8 + 1) * 128], in_=win[24 + g8]),
                      writes=[("wz", g8)], dkey=("wz", g8))
            P.add("sp", lambda e: e.dma_start(out=wsTf.rearrange("p a b -> p (a b)"), in_=wst_in), writes=["wsTf"], dkey="sg0")
            P.add("sp", lambda e: e.dma_start(out=bsv, in_=bs_in), writes=["bsv"], dkey="sg1")
            P.add("sp", lambda e: e.dma_start(out=lng, in_=lng_in), writes=["lng"], dkey="sg2")
            P.add("sp", lambda e: e.dma_start(out=lnb, in_=lnb_in), writes=["lnb"], dkey="sg3")
            P.add("dve", lambda e: e.tensor_copy(out=wsTb.rearrange("p a b -> p (a b)"), in_=wsTf.rearrange("p a b -> p (a b)")),
                  reads=["wsTf"], writes=["wsTb"])
            def zmm(t):
                zb = 2 * (t % 2)
                for half in range(2):
                    for k in range(NK):
                        P.add("pe", lambda e, t=t, k=k, half=half, zb=zb: e.matmul(
                            bank(zb + half, 512), lhsT=xn2T[:, k, t * 128:(t + 1) * 128], rhs=wz[:, k, half * 512:(half + 1) * 512],
                            start=(k == 0), stop=(k == NK - 1)),
                            reads=[("wz", g8) for g8 in range(4 * half, 4 * half + 4)] + [("xn2T", k, b3) for b3 in range(3)],
                            writes=[("ps", zb + half)])

            zmm(0)
            for t in range(16):
                b = t % 2
                zb = 2 * b
                if t + 1 < 16:
                    zmm(t + 1)
                zz = ps[:, zb * 512:(zb + 2) * 512]
                zr = [("ps", zb), ("ps", zb + 1)]
                P.add("act", lambda e, b=b, zz=zz: e.activation(out=t1[b], in_=zz, func=AF.Square), reads=zr, writes=[("t1", b)])
                P.add("dve", lambda e, b=b: e.tensor_scalar(out=t1[b], in0=t1[b], scalar1=0.044715, scalar2=1.0,
                                                            op0=ALU.mult, op1=ALU.add), reads=[("t1", b)], writes=[("t1", b)])
                P.add("dve", lambda e, b=b, zz=zz: e.tensor_tensor(out=t1[b], in0=zz, in1=t1[b], op=ALU.mult),
                      reads=zr + [("t1", b)], writes=[("t1", b)])
                P.add("act", lambda e, b=b: e.activation(out=t1[b], in_=t1[b], func=AF.Sigmoid, scale=1.5957691216),
                      reads=[("t1", b)], writes=[("t1", b)])
                P.add("dve", lambda e, b=b, zz=zz: e.tensor_tensor(out=ge[b], in0=zz, in1=t1[b], op=ALU.mult),
                      reads=zr + [("t1", b)], writes=[("ge", b)])
                sm, smr = stat()
                s2, s2r = stat()
                for g in range(4):
                    P.add("dve", lambda e, b=b, g=g, sm=sm: e.tensor_scalar(
                        out=t1[b][:, g * 128:(g + 1) * 128], in0=ge[b][:, 512 + g * 128:512 + (g + 1) * 128],
                        scalar1=1.0, scalar2=None, op0=ALU.mult, op1=ALU.add, accum_out=sm[:, g:g + 1]),
                        reads=[("ge", b), ("t1", b)], writes=[("t1", b), smr])
                P.add("dve", lambda e, sm=sm: e.tensor_scalar(out=sm[:, 0:4], in0=sm[:, 0:4], scalar1=1.0 / 128, scalar2=None,
                                                              op0=ALU.mult), reads=[smr], writes=[smr])
                for g in range(4):
                    P.add("dve", lambda e, b=b, g=g, sm=sm: e.tensor_scalar(
                        out=t1[b][:, g * 128:(g + 1) * 128], in0=ge[b][:, 512 + g * 128:512 + (g + 1) * 128],
                        scalar1=sm[:, g:g + 1], scalar2=None, op0=ALU.subtract),
                        reads=[("ge", b), smr, ("t1", b)], writes=[("t1", b)])
                    P.add("act", lambda e, b=b, g=g, s2=s2: e.activation(
                        out=t1[b][:, 512 + g * 128:512 + (g + 1) * 128], in_=t1[b][:, g * 128:(g + 1) * 128],
                        func=AF.Square, accum_out=s2[:, g:g + 1]),
                        reads=[("t1", b)], writes=[("t1", b), s2r])
                P.add("dve", lambda e, s2=s2: e.tensor_scalar(out=s2[:, 0:4], in0=s2[:, 0:4], scalar1=1.0 / 128, scalar2=EPS,
                                                              op0=ALU.mult, op1=ALU.add), reads=[s2r], writes=[s2r])
                P.add("act", lambda e, s2=s2: e.activation(out=s2[:, 0:4], in_=s2[:, 0:4], func=AF.Sqrt), reads=[s2r], writes=[s2r])
                P.add("dve", lambda e, s2=s2: e.reciprocal(out=s2[:, 0:4], in_=s2[:, 0:4]), reads=[s2r], writes=[s2r])
                for g in range(4):
                    P.add("dve", lambda e, b=b, g=g, s2=s2: e.scalar_tensor_tensor(
                        out=t1[b][:, g * 128:(g + 1) * 128], in0=t1[b][:, g * 128:(g + 1) * 128], scalar=s2[:, g:g + 1],
                        in1=lng[:, g * 128:(g + 1) * 128], op0=ALU.mult, op1=ALU.mult),
                        reads=[("t1", b), s2r, "lng"], writes=[("t1", b)])
                P.add("dve", lambda e, b=b: e.tensor_tensor(out=vln[b], in0=t1[b][:, 0:512], in1=lnb, op=ALU.add),
                      reads=[("t1", b), "lnb"], writes=[("vln", b)])
                mb = 4 + b
                for g in range(4):
                    P.add("pe", lambda e, b=b, g=g, mb=mb: e.matmul(
                        bank(mb, 128, g * 128), lhsT=wsTb[:, g, :], rhs=vln[b][:, g * 128:(g + 1) * 128], start=True, stop=True),
                        reads=["wsTb", ("vln", b)], writes=[("ps", mb)])
                for g in range(4):
                    P.add("dve", lambda e, b=b, g=g, mb=mb: e.scalar_tensor_tensor(
                        out=ysg[b][:, g * 128:(g + 1) * 128], in0=bank(mb, 128, g * 128), scalar=bsv[:, g:g + 1],
                        in1=ge[b][:, g * 128:(g + 1) * 128], op0=ALU.add, op1=ALU.mult),
                        reads=[("ps", mb), "bsv", ("ge", b)], writes=[("ysg", b)])
                P.add("sp", lambda e, t=t, b=b: e.dma_start(out=ys[t * 128:(t + 1) * 128, 1024:1536], in_=ysg[b]),
                      reads=[("ysg", b)], writes=[("ys_sg", t)], dkey=("ysg", b))
            P.barrier()
            A.top = mark0

        def wout_phase():
            mark = A.top
            TB = 512
            NB_ = NOWN // TB
            woutb = A.alloc([NK, D], BF16)
            yt = [A.alloc([D], F32) for _ in range(2)]
            ysb = [A.alloc([D], BF16) for _ in range(2)]
            yT = [A.alloc([NK, TB], BF16) for _ in range(2)]
            sq = [A.alloc([TB], F32) for _ in range(2)]
            accv = A.alloc([TB], F32)
            rbc = rbcBp
            fT = [A.alloc([TB], F32) for _ in range(2)]
            fT2 = fTp
            hTt = hTtp
            for k in range(NK):
                P.add("pool", lambda e, k=k: e.dma_start(out=woutb[:, k, :], in_=wout[k * 128:(k + 1) * 128, :]),
                      writes=[("woutb", k)], dkey=("woutb", k))
            secs = [(0, 1024), (1024, 512), (1536, 512)]

            def front(blk, tt):
                t = blk * 4 + tt
                b = t % 2
                P.add("sp", lambda e, t=t, b=b: e.dma_start(out=yt[b], in_=ys[t * 128:(t + 1) * 128, :]),
                      writes=[("yt", b)], dkey=("yt", b))
                ss, ssr = stat()
                for si, (s0, sn) in enumerate(secs):
                    P.add("act", lambda e, b=b, si=si, s0=s0, sn=sn, ss=ss: e.activation(
                        out=ysb[b][:, s0:s0 + sn], in_=yt[b][:, s0:s0 + sn], func=AF.Square, accum_out=ss[:, si:si + 1]),
                        reads=[("yt", b)], writes=[("ysb", b), ssr])
                    P.add("dve", lambda e, si=si, sn=sn, ss=ss: e.tensor_scalar(
                        out=ss[:, si:si + 1], in0=ss[:, si:si + 1], scalar1=1.0 / sn, scalar2=EPS, op0=ALU.mult, op1=ALU.add),
                        reads=[ssr], writes=[ssr])
                P.add("act", lambda e, ss=ss: e.activation(out=ss[:, 0:3], in_=ss[:, 0:3], func=AF.Sqrt), reads=[ssr], writes=[ssr])
                P.add("dve", lambda e, ss=ss: e.reciprocal(out=ss[:, 0:3], in_=ss[:, 0:3]), reads=[ssr], writes=[ssr])
                for si, (s0, sn) in enumerate(secs):
                    P.add("dve", lambda e, b=b, si=si, s0=s0, sn=sn, ss=ss: e.tensor_scalar(
                        out=ysb[b][:, s0:s0 + sn], in0=yt[b][:, s0:s0 + sn], scalar1=ss[:, si:si + 1], scalar2=None, op0=ALU.mult),
                        reads=[("yt", b), ssr], writes=[("ysb", b)])

            def back(blk, tt):
                t = blk * 4 + tt
                b = t % 2
                yb = blk % 2
                for k in range(NK):
                    bk = 6 + k // 8
                    P.add("pe", lambda e, b=b, k=k, bk=bk: e.transpose(
                        out=bank(bk).bitcast(BF16)[:, (k % 8) * 128:(k % 8 + 1) * 128],
                        in_=ysb[b][:, k * 128:(k + 1) * 128], identity=identb),
                        reads=[("ysb", b), "identb"], writes=[("ps", bk)])
                P.add("dve", lambda e, tt=tt, yb=yb: e.tensor_tensor(
                    out=yT[yb][:, :, tt * 128:(tt + 1) * 128],
                    in0=ps[:, 6 * 512:8 * 512].bitcast(BF16).rearrange("p (k t) -> p k t", t=128),
                    in1=gall[:, G_OUTN, :].unsqueeze(2).to_broadcast([128, NK, 128]), op=ALU.mult),
                    reads=[("ps", 6), ("ps", 7), "gall"], writes=[("yT", yb, tt)])

            for tt in range(4):
                front(0, tt)
                back(0, tt)
            pending = []
            for blk in range(NB_):
                def hook(dc, blk=blk):
                    if pending:
                        pending.pop(0)()
                    if blk + 1 < NB_:
                        if dc % 4 == 0:
                            front(blk + 1, dc // 4)
                        elif dc % 4 == 2:
                            back(blk + 1, dc // 4)

                yb = blk % 2
                steps = proj_postnorm(TB, NK,
                                      lambda dc, c: woutb[:, c, dc * 128:(dc + 1) * 128],
                                      lambda c, bs, bn, yb=yb: yT[yb][:, c, bs:bs + bn],
                                      lambda dc, c, bi, yb=yb: [("woutb", c)] + [("yT", yb, tt) for tt in range(4)],
                                      None, h1T, "h1T", blk * TB, h2T, "h2T", G_MIXPOST, 1.0,
                                      (sq, accv, rbc, fT, hTt), defer=True, hook=hook, fT2=fT2)
                while pending:
                    pending.pop(0)()
                pending = steps
            P.barrier()
            A.top = mark
            return pending

        STAGES = ["fm", "pre", "ffn1", "mix", "wout", "ffn2", "all"]
        sub = None
        if stop_after in ("m2a", "m2b", "m2c", "m2d"):
            DBG["cut"] = stop_after[2]
            stop_after = "m2"
        if stop_after in ("m0", "m1", "m2", "m3", "m3a", "mixonly"):
            DBG["stop"] = stop_after
            DBG["skip_ffn1"] = True
            if stop_after == "m3a":
                DBG["stop"] = "m3"
                DBG["nheads"] = 1
            stop_after = "mix"
        if stop_after in ("dn", "dn2", "p1"):
            DBG["stop"] = stop_after
            stop_after = "ffn1"
        if stop_after in ("gu",):
            sub = stop_after
            stop_after = "ffn1"
        lvl = STAGES.index(stop_after) if stop_after is not None else len(STAGES) - 1
        to_featmajor(x_in, h0T, "h0T", NT)
        to_featmajor(mem_in, memT0, "memT0", NMEM)
        if lvl == 1:
            mark = A.top
            hT_ = A.alloc([NK, 768], F32)
            xn_ = A.alloc([NK, 768], BF16)
            xf_ = A.alloc([NK, 768], F32)
            tmp_ = ([A.alloc([768], F32) for _ in range(2)], A.alloc([768], F32), A.alloc([768], F32))
            prenorm_T(h0T, "h0T", 0, 768, G_F1PRE, hT_, xn_, lambda k: ("xn_", k), tmp_)
            P.add("dve", lambda e: e.tensor_copy(out=xf_.rearrange("p a b -> p (a b)"), in_=xn_.rearrange("p a b -> p (a b)")),
                  reads=[("xn_", k) for k in range(NK)], writes=["xf_"])
            P.add("sp", lambda e: e.dma_start(out=h1T[:, :, 0:768].rearrange("k p t -> p k t"), in_=xf_), reads=["xf_"],
                  writes=["h1Tdbg"], dkey="dbg")
        pend = []
        if lvl >= 2 and not DBG.get("skip_ffn1"):
            pend = ffn_phase(h0T, "h0T", h1T, "h1T", [(0, 768), (768, 768), (1536, 768)], wgu1, wd1, G_F1PRE, G_F1POST, dbg_stop=sub)
        if lvl >= 3:
            mixer_phase(pend)
            pend = []
        if lvl >= 4:
            pend = wout_phase()
        if lvl >= 5:
            pend = ffn_phase(h2T, "h2T", h3T, "h3T", [(0, 768), (768, 640), (1408, 640)], wgu2, wd2, G_F2PRE, G_F2POST,
                             pre_pending=pend)
        if lvl >= 6:
            final_phase(h3T, "h3T", pend)
            pend = []
        for st_ in (pend or []):
            st_()
        P.barrier()
        P.emit(nc)
    return nc


def _prep_shared(inp):
    f = np.float32
    sh = {}

    def gu(w):
        w = np.asarray(w, f).reshape(NK, 128, 2, NCH, 128)
        return np.ascontiguousarray(w.transpose(3, 1, 0, 2, 4)).reshape(NCH, 128, NK, 256)

    def dn(w):
        w = np.asarray(w, f).reshape(NCH, 128, NK, 128)
        return np.ascontiguousarray(w.transpose(2, 1, 0, 3))

    sh["wgu1"] = gu(inp["ffn1_w_gate_up"][0])
    sh["wd1"] = dn(inp["ffn1_w_down"][0])
    sh["wgu2"] = gu(inp["ffn2_w_gate_up"][0])
    sh["wd2"] = dn(inp["ffn2_w_down"][0])
    w = np.asarray(inp["w_in"][0], f).reshape(NK, 128, 36, 128)
    sh["win"] = np.ascontiguousarray(w.transpose(2, 1, 0, 3))
    sh["wkv"] = np.ascontiguousarray(np.asarray(inp["w_mem_kv"][0], f))
    sh["wout"] = np.ascontiguousarray(np.asarray(inp["w_out"][0], f))
    outn = np.concatenate([inp["out_norm_na"][0], inp["out_norm_sg"][0], inp["out_norm_mem"][0]])
    gl = [inp["ffn1_norm_pre"][0], inp["ffn1_norm_post"][0], inp["mix_norm_pre"][0], inp["mem_norm"][0], outn,
          inp["mix_norm_post"][0], inp["ffn2_norm_pre"][0], inp["ffn2_norm_post"][0], inp["final_norm"][0]]
    gall = np.stack([np.asarray(g, f).reshape(NK, 128).T for g in gl], axis=1)
    sh["gall"] = np.ascontiguousarray(gall).reshape(128, 9 * 16)
    ws = np.asarray(inp["sg_w_spatial"][0], f)
    sh["wst"] = np.ascontiguousarray(ws.transpose(2, 0, 1)).reshape(128, 512)
    sh["bs"] = np.ascontiguousarray(np.asarray(inp["sg_b_spatial"][0], f).T)
    sh["lng"] = np.ascontiguousarray(np.broadcast_to(np.asarray(inp["sg_ln_gain"][0], f).reshape(1, 512), (128, 512)))
    sh["lnb"] = np.ascontiguousarray(np.broadcast_to(np.asarray(inp["sg_ln_bias"][0], f).reshape(1, 512), (128, 512)))
    sh["ident"] = np.eye(128, dtype=f)
    rpb = np.asarray(inp["na_rpb"][0], f)
    sh["bias_lo"] = build_bias_table(rpb, False)
    sh["bias_hi"] = build_bias_table(rpb, True)
    return sh


def make_in_maps(inp):
    sh = _prep_shared(inp)
    x = np.asarray(inp["x"], np.float32)
    mem = np.asarray(inp["mem"], np.float32)
    maps = []
    for core in range(8):
        b, half = core // 2, core % 2
        if half == 0:
            xl = x[b, 0:NT]
        else:
            xl = np.concatenate([x[b, 2048:4096], x[b, 1792:2048]], axis=0)
        m = {k: v for k, v in sh.items() if not k.startswith("bias_")}
        m["bias"] = sh["bias_hi"] if half else sh["bias_lo"]
        m["x"] = np.ascontiguousarray(xl)
        m["mem"] = np.ascontiguousarray(mem[b])
        maps.append(m)
    return maps


_NC_CACHE = {}


def kernel(**inputs):
    maps = make_in_maps(inputs)
    if "nc" not in _NC_CACHE:
        _NC_CACHE["nc"] = build_program()
    nc = _NC_CACHE["nc"]
    res = run_bass_kernel_spmd(nc, maps, core_ids=list(range(8)))
    outp = np.empty((4, 4096, D), dtype=np.float32)
    for core in range(8):
        b, half = core // 2, core % 2
        outp[b, half * 2048:(half + 1) * 2048] = res.results[core]["out"]
    return outp
```
